# Optimizing a Trainium2 kernel written in Bass

```python
import jax, jax.numpy as jnp
from jax import lax
import numpy as np

D_MODEL = 1024
BATCH = 32
SEQ = 256
DEPTH = 4
DEC_BATCH = 8
DEC_SEQ = 4096
PAST_LEN = 512

N_HEADS = 16
N_KV_HEADS = 4
HEAD_DIM = D_MODEL // N_HEADS
KV_GROUP = N_HEADS // N_KV_HEADS
D_FF = 2816
GRID_W = 64
Q_BLOCK = 128
WINDOW = 128
ROPE_FREQS = HEAD_DIM // 4
ROPE_THETA = 10000.0
N_MOD = 9
N_MIXERS = 2
ATTN_SCALE = HEAD_DIM ** -0.5
EPS = 1e-6
NEG_INF = -1e30

kernel_name = "hybrid_dit_prefix_gqa_swa_macaron_step"


def _rmsnorm(x, g):
    xf = x.astype(jnp.float32)
    y = xf * lax.rsqrt(jnp.mean(xf * xf, axis=-1, keepdims=True) + EPS)
    return (y * g.astype(jnp.float32)).astype(x.dtype)


def _modulation(cond, w, b):
    m = jax.nn.silu(cond) @ w + b
    return jnp.split(m[:, None, :], N_MOD, axis=-1)


def _modln(h, g, shift, scale):
    return _rmsnorm(h, g) * (1 + scale) + shift


def _swiglu(x, w_in, w_out):
    gate, up = jnp.split(x @ w_in, 2, axis=-1)
    return (jax.nn.silu(gate) * up) @ w_out


def _qkv(a, w_qkv, qg, kg):
    b, l, _ = a.shape
    proj = a @ w_qkv
    q, k, v = jnp.split(proj, [N_HEADS * HEAD_DIM, (N_HEADS + N_KV_HEADS) * HEAD_DIM], axis=-1)
    q = _rmsnorm(q.reshape(b, l, N_HEADS, HEAD_DIM), qg)
    k = _rmsnorm(k.reshape(b, l, N_KV_HEADS, HEAD_DIM), kg)
    v = v.reshape(b, l, N_KV_HEADS, HEAD_DIM)
    return q, k, v


def _axial_rope(x):
    n_tok = x.shape[1]
    rows = n_tok // GRID_W
    row = jnp.repeat(jnp.arange(rows), GRID_W).astype(jnp.float32)
    col = jnp.tile(jnp.arange(GRID_W), rows).astype(jnp.float32)
    freqs = 1.0 / jnp.power(ROPE_THETA, jnp.arange(ROPE_FREQS, dtype=jnp.float32) / ROPE_FREQS)
    ang = jnp.stack([row[:, None] * freqs, col[:, None] * freqs], axis=1)
    cos = jnp.cos(ang)[:, None]
    sin = jnp.sin(ang)[:, None]
    xf = x.astype(jnp.float32).reshape(x.shape[:-1] + (2, 2, ROPE_FREQS))
    x1, x2 = xf[..., 0, :], xf[..., 1, :]
    out = jnp.stack([x1 * cos - x2 * sin, x2 * cos + x1 * sin], axis=-2)
    return out.reshape(x.shape).astype(x.dtype)


def _softmax_sink(s, sink):
    if sink is None:
        return jax.nn.softmax(s, axis=-1)
    sk = sink.astype(jnp.float32).reshape(N_KV_HEADS, KV_GROUP)[None, :, :, None, None]
    m = jnp.maximum(jnp.max(s, axis=-1, keepdims=True), sk)
    e = jnp.exp(s - m)
    return e / (jnp.sum(e, axis=-1, keepdims=True) + jnp.exp(sk - m))


def _attend_blocks(q, k, v, sink):
    b, lq, h, d = q.shape
    nb = lq // Q_BLOCK
    qb = q.reshape(b, nb, Q_BLOCK, N_KV_HEADS, KV_GROUP, d).transpose(1, 0, 2, 3, 4, 5)

    def one(qi):
        s = jnp.einsum("bqkgd,bskd->bkgqs", qi, k, preferred_element_type=jnp.float32) * ATTN_SCALE
        p = _softmax_sink(s, sink)
        o = jnp.einsum("bkgqs,bskd->bqkgd", p.astype(v.dtype), v)
        return o.reshape(b, Q_BLOCK, h, d)

    o = lax.map(one, qb)
    return o.transpose(1, 0, 2, 3, 4).reshape(b, lq, h, d)


def _attend_window(q, k_lat, v_lat, k_ctx, v_ctx, sink):
    b, l, h, d = q.shape
    nb = l // Q_BLOCK
    band = Q_BLOCK + 2 * WINDOW
    c_len = k_ctx.shape[1]
    pad = ((0, 0), (WINDOW, WINDOW), (0, 0), (0, 0))
    kp = jnp.pad(k_lat, pad)
    vp = jnp.pad(v_lat, pad)
    qb = q.reshape(b, nb, Q_BLOCK, N_KV_HEADS, KV_GROUP, d).transpose(1, 0, 2, 3, 4, 5)

    def one(args):
        qi, bi = args
        start = bi * Q_BLOCK
        kb = lax.dynamic_slice_in_dim(kp, start, band, axis=1)
        vb = lax.dynamic_slice_in_dim(vp, start, band, axis=1)
        s_band = jnp.einsum("bqkgd,bskd->bkgqs", qi, kb, preferred_element_type=jnp.float32) * ATTN_SCALE
        qpos = start + jnp.arange(Q_BLOCK)
        kpos = start - WINDOW + jnp.arange(band)
        valid = (jnp.abs(qpos[:, None] - kpos[None, :]) <= WINDOW) & (kpos >= 0)[None, :] & (kpos < l)[None, :]
        s_band = jnp.where(valid, s_band, NEG_INF)
        s_ctx = jnp.einsum("bqkgd,bskd->bkgqs", qi, k_ctx, preferred_element_type=jnp.float32) * ATTN_SCALE
        p = _softmax_sink(jnp.concatenate([s_ctx, s_band], axis=-1), sink).astype(v_lat.dtype)
        o = (jnp.einsum("bkgqs,bskd->bqkgd", p[..., :c_len], v_ctx)
             + jnp.einsum("bkgqs,bskd->bqkgd", p[..., c_len:], vb))
        return o.reshape(b, Q_BLOCK, h, d)

    o = lax.map(one, (qb, jnp.arange(nb)))
    return o.transpose(1, 0, 2, 3, 4).reshape(b, l, h, d)


def _macaron_layer(h, mods, norm_g_i, w_in_i, w_out_i, mix):
    sh1, sc1, g1, sh2, sc2, g2, sh3, sc3, g3 = mods
    h = h + 0.5 * g1 * _swiglu(_modln(h, norm_g_i[0], sh1, sc1), w_in_i[0], w_out_i[0])
    out, aux = mix(_modln(h, norm_g_i[1], sh2, sc2))
    h = h + g2 * out
    h = h + 0.5 * g3 * _swiglu(_modln(h, norm_g_i[2], sh3, sc3), w_in_i[1], w_out_i[1])
    return h, aux


def setup_inputs(seed: int = 0) -> dict:
    key = jax.random.key(seed)
    ks = jax.random.split(key, 16)
    f32 = jnp.float32

    def nrm(k, shape, scale):
        return jax.random.normal(k, shape, f32) * scale

    n_win = DEPTH // N_MIXERS
    qkv_cols = (N_HEADS + 2 * N_KV_HEADS) * HEAD_DIM
    kv_shape = (DEC_BATCH, DEPTH, PAST_LEN, N_KV_HEADS, HEAD_DIM)
    return {
        "x_prompt": nrm(ks[0], (BATCH, SEQ, D_MODEL), 1.0),
        "x_sample": nrm(ks[1], (DEC_BATCH, DEC_SEQ, D_MODEL), 1.0),
        "cache_k": nrm(ks[2], kv_shape, 1.0),
        "cache_v": nrm(ks[3], kv_shape, 1.0),
        "c": nrm(ks[4], (DEC_BATCH, D_MODEL), 1.0),
        "c_ctx": nrm(ks[5], (D_MODEL,), 1.0),
        "w_mod": nrm(ks[6], (DEPTH, D_MODEL, N_MOD * D_MODEL), 0.5 * D_MODEL ** -0.5),
        "b_mod": nrm(ks[7], (DEPTH, N_MOD * D_MODEL), 0.01),
        "norm_g": 1.0 + nrm(ks[8], (DEPTH, 3, D_MODEL), 0.02),
        "w_qkv": nrm(ks[9], (DEPTH, D_MODEL, qkv_cols), D_MODEL ** -0.5),
        "w_o": nrm(ks[10], (DEPTH, N_HEADS * HEAD_DIM, D_MODEL), (N_HEADS * HEAD_DIM) ** -0.5),
        "q_norm_g": 1.0 + nrm(ks[11], (DEPTH, HEAD_DIM), 0.02),
        "k_norm_g": 1.0 + nrm(ks[12], (DEPTH, HEAD_DIM), 0.02),
        "sink": nrm(ks[13], (n_win, N_HEADS), 0.5),
        "w_ffn_in": nrm(ks[14], (DEPTH, 2, D_MODEL, 2 * D_FF), D_MODEL ** -0.5),
        "w_ffn_out": nrm(ks[15], (DEPTH, 2, D_FF, D_MODEL), D_FF ** -0.5),
    }


def reference(x_prompt, x_sample, cache_k, cache_v, c, c_ctx, w_mod, b_mod, norm_g, w_qkv, w_o,
              q_norm_g, k_norm_g, sink, w_ffn_in, w_ffn_out):
    h = x_prompt
    bp, lp, _ = x_prompt.shape
    new_ks, new_vs = [], []
    for i in range(DEPTH):
        sink_i = sink[i // N_MIXERS] if i % N_MIXERS == 1 else None
        mods = _modulation(c_ctx[None, :], w_mod[i], b_mod[i])

        def mix_ctx(a, i=i, sink_i=sink_i):
            q, k, v = _qkv(a, w_qkv[i], q_norm_g[i], k_norm_g[i])
            o = _attend_blocks(q, k, v, sink_i)
            return o.reshape(bp, lp, N_HEADS * HEAD_DIM) @ w_o[i], (k, v)

        h, (k_i, v_i) = _macaron_layer(h, mods, norm_g[i], w_ffn_in[i], w_ffn_out[i], mix_ctx)
        new_ks.append(k_i)
        new_vs.append(v_i)
    y_prompt = h
    new_k = jnp.stack(new_ks, axis=1)
    new_v = jnp.stack(new_vs, axis=1)

    h = x_sample
    bs, ls, _ = x_sample.shape
    for i in range(DEPTH):
        mods = _modulation(c, w_mod[i], b_mod[i])
        ck = cache_k[:, i]
        cv = cache_v[:, i]

        def mix_lat(a, i=i, ck=ck, cv=cv):
            q, k, v = _qkv(a, w_qkv[i], q_norm_g[i], k_norm_g[i])
            q = _axial_rope(q)
            k = _axial_rope(k)
            if i % N_MIXERS == 0:
                o = _attend_blocks(q, jnp.concatenate([ck, k], axis=1), jnp.concatenate([cv, v], axis=1), None)
            else:
                o = _attend_window(q, k, v, ck, cv, sink[i // N_MIXERS])
            return o.reshape(bs, ls, N_HEADS * HEAD_DIM) @ w_o[i], None

        h, _ = _macaron_layer(h, mods, norm_g[i], w_ffn_in[i], w_ffn_out[i], mix_lat)
    y_sample = h

    return (y_prompt, y_sample, new_k, new_v)
```

```python
import numpy as np
from contextlib import ExitStack
import concourse.bass as bass
import concourse.mybir as mybir
from concourse.bass_utils import run_bass_kernel_spmd

F32 = mybir.dt.float32
BF16 = mybir.dt.bfloat16
U8 = mybir.dt.uint8
AF = mybir.ActivationFunctionType
ALU = mybir.AluOpType

P = 128
D = 1024
KC = 8
DFF = 2816
JC = 22
NH = 16
NKV = 4
HD = 64
TT = 512
NS = 4096
NPR = 1024
NTOK = NS + NPR
NT = NTOK // TT
CTX = 512
DEPTH = 4
QKVC = 1536
NKT = (CTX + NS) // P
EPS = 1e-6
GRAN = 256
NDQ = 8
N_CORES = 8

ENGS = ("pe", "act", "dve", "pool", "sp")


class Op:
    __slots__ = ("eng", "fn", "deps", "signal", "idx", "dma", "dseq")

    def __init__(self, eng, fn, dma):
        self.eng = eng
        self.fn = fn
        self.dma = dma
        self.deps = ()
        self.signal = False
        self.idx = 0
        self.dseq = -1


class Sched:
    def __init__(self):
        self.streams = {e: [] for e in ENGS}
        self.lastw = {}
        self.readers = {}
        self.dma_count = {e: 0 for e in ENGS}

    def add(self, eng, fn, reads=(), writes=(), dma=False):
        o = Op(eng, fn, dma)
        pr = [k for k in reads if k[0] == "P"]
        if pr:
            reads = [k for k in reads if k[0] != "P"]
            writes = list(writes) + pr
        raw = set()
        war = set()
        for k in reads:
            w = self.lastw.get(k)
            if w is not None:
                raw.add(w)
        for k in writes:
            w = self.lastw.get(k)
            if w is not None:
                raw.add(w)
            rd = self.readers.get(k)
            if rd:
                war.update(rd.values())
        real = []
        for d in raw:
            if d.dma or dma:
                real.append(d)
            elif d.eng == eng:
                if eng != "pe":
                    real.append(d)
            else:
                real.append(d)
        for d in war:
            if d in raw:
                continue
            if d.dma or dma:
                real.append(d)
            elif d.eng != eng:
                real.append(d)
        for d in real:
            d.signal = True
        o.deps = real
        for k in reads:
            rd = self.readers.get(k)
            if rd is None:
                rd = self.readers[k] = {}
            rd[("d", id(o)) if dma else eng] = o
        for k in writes:
            self.lastw[k] = o
            self.readers[k] = {}
        if dma:
            o.dseq = self.dma_count[eng]
            self.dma_count[eng] += 1
            o.signal = True
        self.streams[eng].append(o)
        return o

    def number(self):
        for e in ENGS:
            c = 0
            for o in self.streams[e]:
                if (not o.dma) and o.signal:
                    c += 1
                    o.idx = c

    def emit(self, eng, E, sems, dsems, final=False):
        waited = {}
        for o in self.streams[eng]:
            need = {}
            for d in o.deps:
                if d.dma:
                    s = dsems[d.eng][d.dseq % NDQ]
                    v = 16 * (d.dseq // NDQ + 1)
                else:
                    s = sems[d.eng]
                    v = d.idx
                cur = need.get(s.name)
                if cur is None or cur[1] < v:
                    need[s.name] = (s, v)
            if o.dma and o.dseq >= NDQ:
                s = dsems[eng][o.dseq % NDQ]
                v = 16 * (o.dseq // NDQ)
                cur = need.get(s.name)
                if cur is None or cur[1] < v:
                    need[s.name] = (s, v)
            for name, (s, v) in need.items():
                if waited.get(name, 0) < v:
                    E.wait_ge(s, v)
                    waited[name] = v
            ins = o.fn(E)
            if o.dma:
                ins.then_inc(dsems[eng][o.dseq % NDQ], 16)
            elif o.signal:
                ins.then_inc(sems[eng], 1)
        if final:
            for q in ENGS:
                n = self.dma_count[q]
                for i in range(min(NDQ, n)):
                    cnt = (n - i + NDQ - 1) // NDQ
                    if cnt > 0:
                        E.wait_ge(dsems[q][i], 16 * cnt)


class Buf:
    arena = None

    def __init__(self, off, shape, dt):
        self.off = off
        self.shape = tuple(shape)
        self.dt = dt
        self.esz = 4 if dt == F32 else 2
        n = 1
        for s in shape:
            n *= s
        self.n = n
        self.nbytes = n * self.esz
        self._ap = None

    @property
    def v(self):
        if self._ap is None:
            a = Buf.arena[:, self.off:self.off + self.nbytes].bitcast(self.dt)
            if len(self.shape) == 2:
                a = a.rearrange("p (a b) -> p a b", a=self.shape[0])
            elif len(self.shape) == 3:
                a = a.rearrange("p (a b c) -> p a b c", a=self.shape[0], b=self.shape[1])
            elif len(self.shape) == 4:
                a = a.rearrange("p (a b c d) -> p a b c d", a=self.shape[0], b=self.shape[1], c=self.shape[2])
            self._ap = a
        return self._ap

    def k(self, lo=0, hi=None):
        if hi is None:
            hi = self.n
        b0 = (self.off + lo * self.esz) // GRAN
        b1 = (self.off + hi * self.esz - 1) // GRAN
        return [("A", g) for g in range(b0, b1 + 1)]

    def ki(self, *idx):
        stride = self.n
        lo = 0
        for d, i in enumerate(idx):
            stride //= self.shape[d]
            lo += i * stride
        return self.k(lo, lo + stride)


class Alloc:
    def __init__(self):
        self.top = 0
        self.peak = 0

    def __call__(self, shape, dt, align=1024):
        off = (self.top + align - 1) // align * align
        b = Buf(off, shape, dt)
        self.top = off + b.nbytes
        self.peak = max(self.peak, self.top)
        return b


def PK(*banks):
    return [("P", b) for b in banks]


def build(depth=DEPTH, dbg=False, phases=("f1", "att", "f2"), mods=True):
    nc = bass.Bass("TRN2", target_bir_lowering=False)
    S = Sched()

    def din(name, shape):
        return nc.dram_tensor(name, list(shape), F32, kind="ExternalInput").ap()

    def dout(name, shape):
        return nc.dram_tensor(name, list(shape), F32, kind="ExternalOutput").ap()

    x_d = din("x", [NTOK, D])
    ck_d = din("ck", [DEPTH, CTX, 256])
    cv_d = din("cv", [DEPTH, CTX, 256])
    cond_d = din("cond", [2, D])
    wmod_d = din("w_mod", [DEPTH, D, 9 * D])
    bmod_d = din("b_mod", [DEPTH, 9 * D])
    ng_d = din("norm_g", [DEPTH * 3, D])
    wqkv_d = din("w_qkv", [DEPTH, D, QKVC])
    wo_d = din("w_o", [DEPTH, D, D])
    qkg_d = din("qk_g", [8, HD])
    sink_d = din("sink", [1, 32])
    win_d = din("w_in", [DEPTH, 2, D, 2 * DFF])
    wout_d = din("w_out", [DEPTH, 2, DFF, D])
    cst_d = din("cst", [P, 6, P])
    rc_d = din("rope_c", [P, NS])
    rs_d = din("rope_s", [P, NS])
    y_d = dout("y", [NTOK, D])
    nk_d = dout("nk", [4, DEPTH, 256, 256])
    nv_d = dout("nv", [4, DEPTH, 256, 256])
    xT_d = nc.dram_tensor("xT_scr", [D, NTOK], F32).ap()
    xT_v = xT_d.rearrange("(k p) t -> p k t", p=P)

    A = Alloc()
    xt = [A([KC, TT], F32) for _ in range(2)]
    xn = A([KC, TT], BF16)
    MODV = A([DEPTH * 2 * 3 * 3, KC], F32)
    identf = A([P], F32)
    cb = A([6, P], BF16)
    epsT = A([1], F32, align=64)
    ones2 = A([2], BF16, align=64)
    es = A([32], F32, align=64)
    qkg = A([8], F32, align=64)
    sd = A([TT], F32)
    rstd = sd
    tring = [A([TT], F32) for _ in range(2)]
    sqring = [A([TT], BF16) for _ in range(3)]
    base = A.top
    Win = A([KC, 2 * DFF], BF16)
    Wout = A([JC, D], BF16)
    hT = A([JC // 2, TT], BF16)
    sg = [A([TT], F32) for _ in range(2)]
    ffn_top = A.top
    A.top = base
    KT = A([2, CTX + NS], BF16)
    VA = A([NKT, NKV, P], BF16)
    Wqkv = A([KC, QKVC], BF16)
    Wo = A([KC, D], BF16)
    QT = A([KC, TT], BF16)
    AT = A([KC, TT], BF16)
    PT = [A([2, TT], BF16) for _ in range(3)]
    osb = [A([TT], F32) for _ in range(2)]
    ropeC = [A([TT], F32) for _ in range(2)]
    ropeS = [A([TT], F32) for _ in range(2)]
    zr = [A([TT], F32) for _ in range(2)]
    zb = [A([TT], BF16) for _ in range(2)]
    ra_off = A.top
    ra = [A([TT], F32) for _ in range(2)]
    rb_off = A.top
    rb = [A([TT], F32) for _ in range(2)]
    sdq = [A([TT], F32) for _ in range(2)]
    rq = sdq
    rec = [A([TT], F32) for _ in range(2)]
    rs_ = rec
    ckst = Buf(osb[0].off, [4, 256], BF16)
    vst = Buf(rb_off, [4, 256], F32)
    nkst = Buf(ra_off, [4, 256], F32)
    att_top = A.top
    A.top = base
    stage = [A([4, D], F32) for _ in range(2)]
    wm = [A([KC, D], BF16) for _ in range(2)]
    bm = [A([D], BF16) for _ in range(2)]
    ctok = A([D], F32)
    ngtok = A([D], F32)
    qktok = A([P], F32)
    scb = A([KC, 2], BF16, align=64)
    ngT = A([KC, 12], F32, align=64)
    modraw = A([72, 2], F32, align=64)
    estmp = A([32], F32, align=64)
    qktmp = A([8], F32, align=64)
    setup_top = A.top
    total = max(ffn_top, att_top, setup_top)
    total = (total + 1023) // 1024 * 1024
    assert total <= 212000, total

    with ExitStack() as ctx:
        arena_t = ctx.enter_context(nc.sbuf_tensor("arena", [P, total], U8))
        ps_t = ctx.enter_context(nc.psum_tensor("ps", [P, 8 * TT], F32))
        sems = {e: ctx.enter_context(nc.semaphore("s_" + e)) for e in ENGS}
        dsems = {e: [ctx.enter_context(nc.semaphore("d_%s%d" % (e, i))) for i in range(NDQ)]
                 for e in ("sp", "pool")}
        for e in ENGS:
            dsems.setdefault(e, dsems["sp"])
        Buf.arena = arena_t[:, :]
        PSb = [ps_t[:, b * TT:(b + 1) * TT] for b in range(8)]

        def PSv(b0, nb):
            return ps_t[:, b0 * TT:(b0 + nb) * TT]

        def mod(l, j, s, w):
            r = ((l * 2 + j) * 3 + s) * 3 + w
            return MODV.v[:, r, :]

        ident_f = identf.v
        ident_b = cb.v[:, 0, :]
        perm_b = cb.v[:, 1, :]
        blk_b = cb.v[:, 2, :]
        ones_b = cb.v[:, 3, :]
        mL = cb.v[:, 4, :]
        mR = cb.v[:, 5, :]

        S.add("sp", lambda e: e.dma_start(out=identf.v, in_=cst_d[:, 0, :]), writes=identf.k(), dma=True)
        S.add("pool", lambda e: e.dma_start(out=cb.v, in_=cst_d), writes=cb.k(), dma=True)
        S.add("pool", lambda e: e.memset(epsT.v, EPS), writes=epsT.k())
        S.add("pool", lambda e: e.memset(ones2.v, 1.0), writes=ones2.k())
        S.add("sp", lambda e: e.dma_start(out=ctok.v[0:2, :], in_=cond_d), writes=ctok.k(), dma=True)
        S.add("sp", lambda e: e.dma_start(out=ngtok.v[0:12, :], in_=ng_d), writes=ngtok.k(), dma=True)
        for r0, c0 in ((0, 0), (0, 64)):
            S.add("sp", lambda e, r0=r0, c0=c0: e.dma_start(out=qktok.v[0:8, c0:c0 + 64], in_=qkg_d),
                  writes=qktok.k(), dma=True)
        S.add("sp", lambda e: e.dma_start(out=estmp.v, in_=sink_d.partition_broadcast(P)),
              writes=estmp.k(), dma=True)
        S.add("act", lambda e: e.activation(out=es.v, in_=estmp.v, func=AF.Exp), reads=estmp.k(), writes=es.k())
        for k in range(KC):
            S.add("pe", lambda e, k=k: e.transpose(PSb[1][:, k * 2:(k + 1) * 2], ctok.v[0:2, k * P:(k + 1) * P],
                                                    ident_f[0:2, 0:2]),
                  reads=ctok.k() + identf.k(), writes=PK(1))
        S.add("act", lambda e: e.activation(out=scb.v, in_=PSb[1][:, 0:16].rearrange("p (k j) -> p k j", j=2),
                                            func=AF.Silu), reads=PK(1), writes=scb.k())
        for k in range(KC):
            S.add("pe", lambda e, k=k: e.transpose(PSb[2][:, k * 12:(k + 1) * 12], ngtok.v[0:12, k * P:(k + 1) * P],
                                                    ident_f[0:12, 0:12]),
                  reads=ngtok.k() + identf.k(), writes=PK(2))
        S.add("dve", lambda e: e.tensor_copy(ngT.v, PSb[2][:, 0:96].rearrange("p (k j) -> p k j", j=12)),
              reads=PK(2), writes=ngT.k())
        S.add("pe", lambda e: e.transpose(PSb[3][:, 0:8], qktok.v[0:8, :], ident_f[0:8, 0:8]),
              reads=qktok.k() + identf.k(), writes=PK(3))
        S.add("dve", lambda e: e.tensor_copy(qktmp.v, PSb[3][:, 0:8]), reads=PK(3), writes=qktmp.k())
        S.add("dve", lambda e: e.tensor_scalar(out=qkg.v[:, 0:4], in0=qktmp.v[:, 0:4], scalar1=HD ** -0.5,
                                               scalar2=None, op0=ALU.mult), reads=qktmp.k(), writes=qkg.k())
        S.add("dve", lambda e: e.tensor_copy(qkg.v[:, 4:8], qktmp.v[:, 4:8]), reads=qktmp.k(), writes=qkg.k())

        def t0_tile(i):
            st = stage[i % 2]
            sl = xt[i % 2]
            S.add("sp", lambda e, i=i, st=st: e.dma_start(
                out=st.v, in_=x_d[i * TT:(i + 1) * TT, :].rearrange("(s p) d -> p s d", p=P)),
                writes=st.k(), dma=True)
            for k in range(KC):
                b = k % 4
                for s in range(4):
                    S.add("pe", lambda e, st=st, k=k, s=s, b=b: e.transpose(
                        PSb[b][:, s * P:(s + 1) * P], st.v[:, s, k * P:(k + 1) * P], ident_f),
                        reads=st.ki(s) + identf.k(), writes=PK(b))
                if k % 2 == 0:
                    S.add("act", lambda e, sl=sl, k=k, b=b: e.copy(sl.v[:, k, :], PSb[b]),
                          reads=PK(b), writes=sl.ki(k))
                else:
                    S.add("dve", lambda e, sl=sl, k=k, b=b: e.tensor_copy(sl.v[:, k, :], PSb[b]),
                          reads=PK(b), writes=sl.ki(k))
            S.add("sp", lambda e, i=i, sl=sl: e.dma_start(out=xT_v[:, :, i * TT:(i + 1) * TT], in_=sl.v),
                  reads=sl.k(), writes=[("X", i)], dma=True)


        def mod_block(l, v):
            mb = 4 + l % 2
            w_ = wm[v % 2]
            b_ = bm[v % 2]
            S.add("pool", lambda e, l=l, v=v, w_=w_: e.dma_start(
                out=w_.v, in_=wmod_d[l, :, v * D:(v + 1) * D].rearrange("(k p) c -> p k c", p=P)),
                writes=w_.k(), dma=True)
            S.add("pool", lambda e, l=l, v=v, b_=b_: e.dma_start(
                out=b_.v[0:1, :], in_=bmod_d[l:l + 1, v * D:(v + 1) * D]),
                writes=b_.k(), dma=True)
            for m in range(KC):
                c = v * KC + m
                for k in range(KC):
                    S.add("pe", lambda e, w_=w_, m=m, k=k, c=c, mb=mb: e.matmul(
                        PSb[mb][:, c * 2:c * 2 + 2], w_.v[:, k, m * P:(m + 1) * P], scb.v[:, k, :],
                        start=(c == 0 and k == 0), stop=False, skip_group_check=True),
                        reads=w_.ki(k) + scb.k(), writes=PK(mb))
                S.add("pe", lambda e, b_=b_, m=m, c=c, mb=mb: e.matmul(
                    PSb[mb][:, c * 2:c * 2 + 2], b_.v[0:1, m * P:(m + 1) * P], ones2.v[0:1, :],
                    start=False, stop=True, skip_group_check=True),
                    reads=b_.k() + ones2.k(), writes=PK(mb))

        def mod_final(l):
            mb = 4 + l % 2
            S.add("dve", lambda e, mb=mb: e.tensor_copy(modraw.v, PSb[mb][:, 0:144].rearrange("p (c j) -> p c j", j=2)),
                  reads=PK(mb), writes=modraw.k())
            for j in range(2):
                for s in range(3):
                    S.add("dve", lambda e, l=l, j=j, s=s: e.scalar_tensor_tensor(
                        out=mod(l, j, s, 0), in0=modraw.v[:, (3 * s + 1) * 8:(3 * s + 2) * 8, j], scalar=1.0,
                        in1=ngT.v[:, :, l * 3 + s], op0=ALU.add, op1=ALU.mult),
                        reads=modraw.k() + ngT.k(), writes=MODV.k())
                    S.add("dve", lambda e, l=l, j=j, s=s: e.tensor_copy(
                        mod(l, j, s, 1), modraw.v[:, (3 * s) * 8:(3 * s + 1) * 8, j]),
                        reads=modraw.k(), writes=MODV.k())
                    S.add("dve", lambda e, l=l, j=j, s=s: e.tensor_scalar(
                        out=mod(l, j, s, 2), in0=modraw.v[:, (3 * s + 2) * 8:(3 * s + 3) * 8, j],
                        scalar1=(1.0 if s == 1 else 0.5), scalar2=None, op0=ALU.mult),
                        reads=modraw.k(), writes=MODV.k())


        mod_work = []
        for l in range(depth if mods else 0):
            for v in range(9):
                mod_work.append(lambda l=l, v=v: mod_block(l, v))
            mod_work.append(lambda l=l: mod_final(l))
        per = (len(mod_work) + NT - 1) // NT
        for i in range(NT):
            t0_tile(i)
            for fn in mod_work[i * per:(i + 1) * per]:
                fn()

        def load_x(i):
            sl = xt[i % 2]
            S.add("sp", lambda e, i=i, sl=sl: e.dma_start(out=sl.v, in_=xT_v[:, :, i * TT:(i + 1) * TT]),
                  reads=[("X", i)], writes=sl.k(), dma=True)

        def store_x(i):
            sl = xt[i % 2]
            S.add("sp", lambda e, i=i, sl=sl: e.dma_start(out=xT_v[:, :, i * TT:(i + 1) * TT], in_=sl.v),
                  reads=sl.k(), writes=[("X", i)], dma=True)

        cnt = {"sq": 0, "t": 0}

        def norm_steps(i, l, s, msb=6):
            j = 0 if i < NS // TT else 1
            sl = xt[i % 2]

            def st_sq(k):
                q = sqring[cnt["sq"] % 3]
                cnt["sq"] += 1
                S.add("act", lambda e: e.activation(out=q.v, in_=sl.v[:, k, :], func=AF.Square),
                      reads=sl.ki(k), writes=q.k())
                S.add("pe", lambda e: e.matmul(PSb[msb], ones_b, q.v, start=(k == 0), stop=(k == KC - 1)),
                      reads=q.k() + cb.k(), writes=PK(msb))

            def st_rs():
                S.add("act", lambda e: e.activation(out=sd.v, in_=PSb[msb], func=AF.Ln, bias=epsT.v, scale=1.0),
                      reads=PK(msb) + epsT.k(), writes=sd.k())
                S.add("act", lambda e: e.activation(out=sd.v, in_=sd.v, func=AF.Exp, scale=-0.5),
                      reads=sd.k(), writes=sd.k())

            def st_xn():
                for k in range(KC):
                    t = tring[cnt["t"] % 2]
                    cnt["t"] += 1
                    S.add("dve", lambda e, k=k, t=t: e.tensor_tensor(out=t.v, in0=sl.v[:, k, :], in1=rstd.v, op=ALU.mult),
                          reads=sl.ki(k) + rstd.k(), writes=t.k())
                    if k % 2 == 0:
                        S.add("pool", lambda e, k=k, t=t: e.tensor_scalar(
                            out=xn.v[:, k, :], in0=t.v, scalar1=mod(l, j, s, 0)[:, k:k + 1], scalar2=mod(l, j, s, 1)[:, k:k + 1],
                            op0=ALU.mult, op1=ALU.add), reads=t.k() + MODV.k(), writes=xn.ki(k))
                    else:
                        S.add("act", lambda e, k=k, t=t: e.activation(
                            out=xn.v[:, k, :], in_=t.v, func=AF.Identity, scale=mod(l, j, s, 0)[:, k:k + 1],
                            bias=mod(l, j, s, 1)[:, k:k + 1]), reads=t.k() + MODV.k(), writes=xn.ki(k))

            return [(lambda k=k: st_sq(k)) for k in range(KC)] + [st_rs, st_xn]

        def norm_tile(i, l, s, msb=6):
            for st in norm_steps(i, l, s, msb):
                st()

        def residual(i, l, s, m, bank):
            j = 0 if i < NS // TT else 1
            sl = xt[i % 2]
            S.add("dve", lambda e, sl=sl, m=m, bank=bank, l=l, j=j, s=s: e.scalar_tensor_tensor(
                out=sl.v[:, m, :], in0=PSb[bank], scalar=mod(l, j, s, 2)[:, m:m + 1], in1=sl.v[:, m, :],
                op0=ALU.mult, op1=ALU.add), reads=PK(bank) + sl.ki(m) + MODV.k(), writes=sl.ki(m))

        def ffn_pass(l, f):
            s = 0 if f == 0 else 2
            HC = DFF // 2
            for hf in range(2):
                for gu_ in range(2):
                    for k in range(KC):
                        c0 = gu_ * DFF + hf * HC
                        S.add("pool", lambda e, k=k, c0=c0: e.dma_start(
                            out=Win.v[:, k, c0:c0 + HC], in_=win_d[l, f, k * P:(k + 1) * P, c0:c0 + HC]),
                            writes=Win.k(k * 2 * DFF + c0, k * 2 * DFF + c0 + HC), dma=True)
                j0 = hf * 11
                S.add("pool", lambda e, j0=j0: e.dma_start(
                    out=Wout.v[:, j0:j0 + 11, :],
                    in_=wout_d[l, f, j0 * P:(j0 + 11) * P, :].rearrange("(j p) n -> p j n", p=P)),
                    writes=Wout.k(j0 * D, (j0 + 11) * D), dma=True)

            HJ = JC // 2

            def gu(i, hf, inter=()):
                for jj in range(HJ):
                    if jj < len(inter):
                        inter[jj]()
                    jc = hf * HJ + jj
                    gb = (jc % 2) * 2
                    ub = gb + 1
                    for (bank, c0) in ((gb, jc * P), (ub, DFF + jc * P)):
                        for k in range(KC):
                            S.add("pe", lambda e, bank=bank, c0=c0, k=k: e.matmul(
                                PSb[bank], Win.v[:, k, c0:c0 + P], xn.v[:, k, :], start=(k == 0), stop=(k == KC - 1)),
                                reads=Win.k(k * 2 * DFF + c0, k * 2 * DFF + c0 + P) + xn.ki(k), writes=PK(bank))
                    g_ = sg[jc % 2]
                    S.add("act", lambda e, g_=g_, gb=gb: e.activation(out=g_.v, in_=PSb[gb], func=AF.Silu),
                          reads=PK(gb), writes=g_.k())
                    S.add("dve", lambda e, g_=g_, ub=ub, jj=jj: e.tensor_tensor(
                        out=hT.v[:, jj, :], in0=PSb[ub], in1=g_.v, op=ALU.mult),
                        reads=PK(ub) + g_.k(), writes=hT.ki(jj))

            def wout(i, hf):
                for m in range(KC):
                    bank = 4 + m % 2
                    for jj in range(HJ):
                        jc = hf * HJ + jj
                        S.add("pe", lambda e, bank=bank, jc=jc, jj=jj, m=m: e.matmul(
                            PSb[bank], Wout.v[:, jc, m * P:(m + 1) * P], hT.v[:, jj, :],
                            start=(jj == 0), stop=(jj == HJ - 1)),
                            reads=Wout.k(jc * D + m * P, jc * D + (m + 1) * P) + hT.ki(jj), writes=PK(bank))
                    residual(i, l, s, m, bank)

            load_x(0)
            norm_tile(0, l, s)
            for i in range(NT):
                nxt = []
                if i + 1 < NT:
                    load_x(i + 1)
                    nxt = norm_steps(i + 1, l, s)
                gu(i, 0)
                wout(i, 0)
                gu(i, 1, nxt[:KC])
                for st in nxt[KC:]:
                    st()
                wout(i, 1)
                store_x(i)

        def att_weights(l):
            S.add("pool", lambda e: e.dma_start(out=Wqkv.v, in_=wqkv_d[l].rearrange("(k p) c -> p k c", p=P)),
                  writes=Wqkv.k(), dma=True)
            S.add("pool", lambda e: e.dma_start(out=Wo.v, in_=wo_d[l].rearrange("(c p) n -> p c n", p=P)),
                  writes=Wo.k(), dma=True)
            S.add("pool", lambda e: e.memset(VA.v[:, :, :, HD:P], 1.0), writes=VA.k())

        def load_ctx(l):
            for t in range(4):
                S.add("pool", lambda e, t=t: e.dma_start(
                    out=VA.v[:, t, :, 0:HD],
                    in_=cv_d[l, t * P:(t + 1) * P, :].rearrange("p (g d) -> p g d", g=NKV)),
                    writes=VA.ki(t), dma=True)
            S.add("pool", lambda e: e.dma_start(out=ckst.v, in_=ck_d[l].rearrange("(t p) c -> p t c", p=P)),
                  writes=ckst.k(), dma=True)
            for t in range(4):
                for i2 in range(2):
                    S.add("pe", lambda e, t=t, i2=i2: e.transpose(
                        PSb[i2].bitcast(BF16)[:, t * P:(t + 1) * P], ckst.v[:, t, i2 * P:(i2 + 1) * P], ident_b),
                        reads=ckst.ki(t) + cb.k(), writes=PK(i2))
            for i2 in range(2):
                S.add("dve", lambda e, i2=i2: e.tensor_copy(KT.v[:, i2, 0:CTX], PSb[i2].bitcast(BF16)[:, 0:CTX]),
                      reads=PK(i2), writes=KT.k(i2 * (CTX + NS), i2 * (CTX + NS) + CTX))

        rcnt = {"c": 0, "r": 0}
        BX_, BY_ = 6, 7

        def load_rope(tok0):
            r = rcnt["r"] % 2
            rcnt["r"] += 1
            S.add("sp", lambda e: e.dma_start(out=ropeC[r].v, in_=rc_d[:, tok0:tok0 + TT]),
                  writes=ropeC[r].k(), dma=True)
            S.add("sp", lambda e: e.dma_start(out=ropeS[r].v, in_=rs_d[:, tok0:tok0 + TT]),
                  writes=ropeS[r].k(), dma=True)
            return r

        def qk_steps(l, wcol, gcol, rope, dst_ap, dst_keys, banks=(6, 7)):
            c = rcnt["c"]
            rcnt["c"] += 1
            r2 = c % 2
            z = zr[r2]
            q = sqring[c % 3]
            BX, BY = banks
            rr = rope

            def s_proj():
                for k in range(KC):
                    S.add("pe", lambda e, k=k: e.matmul(PSb[BX], Wqkv.v[:, k, wcol:wcol + P], xn.v[:, k, :],
                                                        start=(k == 0), stop=(k == KC - 1)),
                          reads=Wqkv.k(k * QKVC + wcol, k * QKVC + wcol + P) + xn.ki(k), writes=PK(BX))

            def s_sq():
                S.add("act", lambda e: e.activation(out=q.v, in_=PSb[BX], func=AF.Square),
                      reads=PK(BX), writes=q.k())

            def s_ms():
                S.add("pe", lambda e: e.matmul(PSb[BY], blk_b, q.v, start=True, stop=True),
                      reads=q.k() + cb.k(), writes=PK(BY))

            def s_rs():
                S.add("act", lambda e: e.activation(out=sdq[r2].v, in_=PSb[BY], func=AF.Ln, bias=epsT.v, scale=1.0),
                      reads=PK(BY) + epsT.k(), writes=sdq[r2].k())
                S.add("act", lambda e: e.activation(out=sdq[r2].v, in_=sdq[r2].v, func=AF.Exp, scale=-0.5),
                      reads=sdq[r2].k(), writes=sdq[r2].k())

            def s_z():
                S.add("dve", lambda e: e.scalar_tensor_tensor(
                    out=z.v, in0=PSb[BX], scalar=qkg.v[:, gcol:gcol + 1], in1=sdq[r2].v, op0=ALU.mult, op1=ALU.mult),
                    reads=PK(BX) + sdq[r2].k() + qkg.k(), writes=z.k())

            if rr is None:
                def s_out():
                    S.add("act", lambda e: e.copy(dst_ap, z.v), reads=z.k(), writes=dst_keys)
                return [s_proj, s_sq, s_ms, s_rs, s_z, s_out], z

            def s_zb():
                S.add("act", lambda e: e.copy(zb[r2].v, z.v), reads=z.k(), writes=zb[r2].k())

            def s_perm():
                S.add("pe", lambda e: e.matmul(PSb[BY], perm_b, zb[r2].v, start=True, stop=True),
                      reads=zb[r2].k() + cb.k(), writes=PK(BY))

            def s_rope():
                S.add("pool", lambda e: e.tensor_tensor(out=ra[r2].v, in0=z.v, in1=ropeC[rr].v, op=ALU.mult),
                      reads=z.k() + ropeC[rr].k(), writes=ra[r2].k())
                S.add("dve", lambda e: e.tensor_tensor(out=rb[r2].v, in0=PSb[BY], in1=ropeS[rr].v, op=ALU.mult),
                      reads=PK(BY) + ropeS[rr].k(), writes=rb[r2].k())
                S.add("dve", lambda e: e.tensor_tensor(out=dst_ap, in0=ra[r2].v, in1=rb[r2].v, op=ALU.add),
                      reads=ra[r2].k() + rb[r2].k(), writes=dst_keys)

            return [s_proj, s_sq, s_ms, s_rs, s_z, s_zb, s_perm, s_rope], z

        def run(steps):
            for st in steps:
                st()

        def run_interleaved(lists, skew=2):
            T = max(len(x) + c * skew for c, x in enumerate(lists))
            for t in range(T):
                for c, x in enumerate(lists):
                    idx = t - c * skew
                    if 0 <= idx < len(x):
                        x[idx]()

        BANKPAIRS = ((6, 7), (0, 1), (2, 3))

        def kv_tile(l, key0, kt0, rope, out_seq0):
            lists = []
            for i2 in range(2):
                lo = i2 * (CTX + NS) + key0
                steps, z = qk_steps(l, D + i2 * P, 4 + l, rope, KT.v[:, i2, key0:key0 + TT], KT.k(lo, lo + TT),
                                    banks=BANKPAIRS[i2])
                if out_seq0 is not None:
                    def s_nk(i2=i2, z=z):
                        tb = 2 + i2
                        for s4 in range(4):
                            S.add("pe", lambda e, s4=s4: e.transpose(
                                PSb[tb][:, s4 * P:(s4 + 1) * P], z.v[:, s4 * P:(s4 + 1) * P], ident_f),
                                reads=z.k() + identf.k(), writes=PK(tb))
                        for gg in range(2):
                            g = i2 + 2 * gg
                            S.add("dve", lambda e, g=g, gg=gg: e.tensor_copy(
                                nkst.v[:, :, g * HD:(g + 1) * HD],
                                PSb[tb].rearrange("p (s c) -> p s c", s=4)[:, :, gg * HD:(gg + 1) * HD]),
                                reads=PK(tb), writes=nkst.k())
                    steps = steps + [s_nk]
                lists.append(steps)

            def s_v():
                for s4 in range(4):
                    vb = 4 + s4 // 2
                    c0 = (s4 % 2) * 256
                    for k in range(KC):
                        S.add("pe", lambda e, s4=s4, k=k, vb=vb, c0=c0: e.matmul(
                            PSb[vb][:, c0:c0 + 256], xn.v[:, k, s4 * P:(s4 + 1) * P], Wqkv.v[:, k, D + 256:QKVC],
                            start=(k == 0), stop=(k == KC - 1), skip_group_check=True),
                            reads=xn.ki(k) + Wqkv.k(k * QKVC + D + 256, (k + 1) * QKVC), writes=PK(vb))
                    S.add("act", lambda e, s4=s4, vb=vb, c0=c0: e.copy(
                        VA.v[:, kt0 + s4, :, 0:HD], PSb[vb][:, c0:c0 + 256].rearrange("p (g d) -> p g d", g=NKV)),
                        reads=PK(vb), writes=VA.ki(kt0 + s4))
                    if out_seq0 is not None:
                        S.add("dve", lambda e, s4=s4, vb=vb, c0=c0: e.tensor_copy(vst.v[:, s4, :], PSb[vb][:, c0:c0 + 256]),
                              reads=PK(vb), writes=vst.ki(s4))
            lists.append([s_v])
            run_interleaved(lists, skew=1)
            if out_seq0 is not None:
                for q2 in range(2):
                    S.add("sp", lambda e, q2=q2: e.dma_start(
                        out=nk_d[out_seq0 + q2, l].rearrange("(h p) c -> p h c", p=P), in_=nkst.v[:, 2 * q2:2 * q2 + 2, :]),
                        reads=nkst.k(), writes=[("NK", out_seq0 + q2, l)], dma=True)
                    S.add("sp", lambda e, q2=q2: e.dma_start(
                        out=nv_d[out_seq0 + q2, l].rearrange("(h p) c -> p h c", p=P), in_=vst.v[:, 2 * q2:2 * q2 + 2, :]),
                        reads=vst.k(), writes=[("NV", out_seq0 + q2, l)], dma=True)

        def q_chunk_steps(l, c, rope, banks=(6, 7)):
            return qk_steps(l, c * P, l, rope, QT.v[:, c, :], QT.ki(c), banks=banks)[0]

        def q_tile_free(l, rope):
            run_interleaved([q_chunk_steps(l, c, rope, BANKPAIRS[c % 3]) for c in range(KC)], skew=2)

        acnt = {"n": 0}

        def attend(l, jobs, use_sink, side=None, act_norm=False):
            items = [(c, jb, n == 0, n == len(jobs) - 1) for c in range(KC) for n, jb in enumerate(jobs)]
            slot = []
            pslot = []
            for _ in items:
                slot.append(acnt["n"] % 2)
                pslot.append(acnt["n"] % 3)
                acnt["n"] += 1

            def s_mm(a):
                c, (kt, key0, qlo, qhi, masks), first, last = items[a]
                g = slot[a]
                i2 = c // 4
                kk0 = i2 * (CTX + NS) + key0
                for h2 in range(2):
                    pr = slice(h2 * HD, (h2 + 1) * HD)
                    bank = 2 * g + h2
                    S.add("pe", lambda e, pr=pr, bank=bank: e.matmul(
                        PSb[bank][:, qlo:qhi], KT.v[pr, i2, key0:key0 + P], QT.v[pr, c, qlo:qhi],
                        start=True, stop=(not masks), skip_group_check=True),
                        reads=KT.k(kk0, kk0 + P) + QT.ki(c), writes=PK(bank))
                    for mi, (blk, side_) in enumerate(masks):
                        mk = mL if side_ == "L" else mR
                        S.add("pe", lambda e, bank=bank, blk=blk, mk=mk, mi=mi: e.matmul(
                            PSb[bank][:, blk * P:(blk + 1) * P], ident_b, mk,
                            start=False, stop=(mi == len(masks) - 1), skip_group_check=True),
                            reads=cb.k(), writes=PK(bank))

            pending = []

            def do_exp(a):
                c, (kt, key0, qlo, qhi, masks), first, last = items[a]
                g = slot[a]
                pt = PT[pslot[a]]
                S.add("act", lambda e: e.activation(
                    out=pt.v[:, :, qlo:qhi], in_=PSv(2 * g, 2).rearrange("p (h q) -> p h q", h=2)[:, :, qlo:qhi],
                    func=AF.Exp), reads=PK(2 * g, 2 * g + 1), writes=pt.k())

            def do_pv(a):
                c, (kt, key0, qlo, qhi, masks), first, last = items[a]
                pt = PT[pslot[a]]
                for h2 in range(2):
                    gk = c // 4 + 2 * h2
                    ob = 4 + h2
                    S.add("pe", lambda e, h2=h2, gk=gk, ob=ob: e.matmul(
                        PSb[ob][:, qlo:qhi], VA.v[:, kt, gk, :], pt.v[:, h2, qlo:qhi],
                        start=first, stop=last, skip_group_check=True),
                        reads=VA.ki(kt, gk) + pt.k(), writes=PK(ob))
                if last:
                    for h2 in range(2):
                        S.add("dve", lambda e, h2=h2: e.tensor_copy(osb[h2].v, PSb[4 + h2]),
                              reads=PK(4 + h2), writes=osb[h2].k())

                    def normalise(c=c):
                        for h2 in range(2):
                            o_ = osb[h2]
                            rc_ = rec[h2]
                            col = (l // 2) * NH + c + 8 * h2
                            if use_sink or act_norm:
                                if use_sink:
                                    S.add("act", lambda e, o_=o_, col=col: e.activation(
                                        out=o_.v[HD:P, :], in_=o_.v[HD:P, :], func=AF.Ln, bias=es.v[HD:P, col:col + 1], scale=1.0),
                                        reads=o_.k() + es.k(), writes=o_.k())
                                else:
                                    S.add("act", lambda e, o_=o_: e.activation(
                                        out=o_.v[HD:P, :], in_=o_.v[HD:P, :], func=AF.Ln),
                                        reads=o_.k(), writes=o_.k())
                                S.add("act", lambda e, o_=o_, rc_=rc_: e.activation(
                                    out=rc_.v[0:HD, :], in_=o_.v[HD:P, :], func=AF.Exp, scale=-1.0),
                                    reads=o_.k(), writes=rc_.k())
                            else:
                                S.add("dve", lambda e, o_=o_, rc_=rc_: e.reciprocal(rc_.v[0:HD, :], o_.v[HD:P, :]),
                                      reads=o_.k(), writes=rc_.k())
                            S.add("dve", lambda e, o_=o_, rc_=rc_, h2=h2: e.tensor_tensor(
                                out=AT.v[h2 * HD:(h2 + 1) * HD, c, :], in0=o_.v[0:HD, :], in1=rc_.v[0:HD, :], op=ALU.mult),
                                reads=o_.k() + rc_.k(), writes=AT.ki(c))
                    pending.append((a + 3, normalise))

            n = len(items)
            for a in range(min(2, n)):
                s_mm(a)
            for a in range(n):
                do_exp(a)
                if a + 2 < n:
                    s_mm(a + 2)
                do_pv(a)
                while pending and pending[0][0] <= a:
                    pending.pop(0)[1]()
                if side and a in side:
                    for fn in side[a]:
                        fn()
            while pending:
                pending.pop(0)[1]()

        def oproj_m(i, l, m):
            bank = BX_ + m % 2
            for c in range(KC):
                S.add("pe", lambda e, c=c: e.matmul(
                    PSb[bank], Wo.v[:, c, m * P:(m + 1) * P], AT.v[:, c, :], start=(c == 0), stop=(c == KC - 1)),
                    reads=Wo.k(c * D + m * P, c * D + (m + 1) * P) + AT.ki(c), writes=PK(bank))
            residual(i, l, 1, m, bank)

        def oproj_tile(i, l):
            for m in range(KC):
                oproj_m(i, l, m)

        def full_jobs(nkt):
            return [(kt, kt * P, 0, TT, []) for kt in range(nkt)]

        def window_jobs(ti):
            jobs = [(kt, kt * P, 0, TT, []) for kt in range(4)]
            for R in range(ti * 4 - 1, ti * 4 + 5):
                if R < 0 or R >= NS // P:
                    continue
                rr = R - ti * 4
                blo, bhi = max(0, rr - 1), min(3, rr + 1)
                masks = []
                for b in range(blo, bhi + 1):
                    if rr == b - 1:
                        masks.append((b, "L"))
                    elif rr == b + 1:
                        masks.append((b, "R"))
                jobs.append((4 + R, CTX + R * P, blo * P, (bhi + 1) * P, masks))
            return jobs

        def prompt_jobs():
            return [(0, 0, 0, 256, []), (1, 128, 0, 256, []), (2, 256, 256, 512, []), (3, 384, 256, 512, [])]

        def attn_layer(l, lvl=9):
            odd = (l % 2 == 1)
            att_weights(l)
            load_ctx(l)
            nst = NS // TT
            load_x(0)
            rr = load_rope(0)
            norm_tile(0, l, 1, BY_)
            for i in range(nst):
                kv_tile(l, CTX + i * TT, 4 + i * 4, rr, None)
                if i + 1 < nst:
                    load_x(i + 1)
                    rr = load_rope((i + 1) * TT)
                    norm_tile(i + 1, l, 1, BY_)
            load_x(0)
            rr = load_rope(0)
            norm_tile(0, l, 1, BY_)
            q_tile_free(l, rr)
            for i in range(nst):
                jobs = window_jobs(i) if odd else full_jobs(NKT)
                nper = len(jobs)
                side = {}

                def at(c, frac, fn):
                    side.setdefault(c * nper + min(nper - 1, int(frac * nper)), []).append(fn)

                if i > 0:
                    for m in range(KC):
                        at(0, 0.05 + 0.05 * m, lambda m=m, i=i: oproj_m(i - 1, l, m))
                    at(0, 0.5, lambda i=i: store_x(i - 1))
                last_steps = None
                if i + 1 < nst:
                    box = {}

                    def ld(i=i, box=box):
                        load_x(i + 1)
                        box["r"] = load_rope((i + 1) * TT)
                    at(0, 0.55, ld)
                    rnext = (rcnt["r"]) % 2
                    nsteps = norm_steps(i + 1, l, 1, BY_)
                    for n_, st in enumerate(nsteps):
                        at(0, 0.55 + 0.04 * n_, st)
                    for c in range(KC):
                        steps = q_chunk_steps(l, c, rnext)
                        if c < KC - 1:
                            for n_, st in enumerate(steps):
                                at(c + 1, 0.05 + 0.1 * n_, st)
                        else:
                            last_steps = steps
                attend(l, jobs, odd, side)
                if last_steps is not None:
                    run(last_steps)
            oproj_tile(nst - 1, l)
            store_x(nst - 1)
            for i in range(nst, NT):
                load_x(i)
            for i in range(nst, NT):
                norm_tile(i, l, 1, BY_)
                kv_tile(l, 0, 0, None, (i - nst) * 2)
                q_tile_free(l, None)
                attend(l, prompt_jobs(), odd, act_norm=True)
                oproj_tile(i, l)
                store_x(i)

        for l in range(depth):
            if "f1" in phases:
                ffn_pass(l, 0)
            for ph_ in phases:
                if ph_.startswith("att"):
                    attn_layer(l, int(ph_[3:]) if len(ph_) > 3 else 9)
            if "f2" in phases:
                ffn_pass(l, 1)

        A.top = base
        load_x(0)
        for i in range(NT):
            sl = xt[i % 2]
            st = stage[i % 2]
            if i + 1 < NT:
                load_x(i + 1)
            for s4 in range(4):
                for hf in range(2):
                    b = (s4 * 2 + hf) % 4
                    for kk in range(4):
                        k = hf * 4 + kk
                        S.add("pe", lambda e, sl=sl, k=k, kk=kk, s4=s4, b=b: e.transpose(
                            PSb[b][:, kk * P:(kk + 1) * P], sl.v[:, k, s4 * P:(s4 + 1) * P], ident_f),
                            reads=sl.ki(k) + identf.k(), writes=PK(b))
                    if hf == 0:
                        S.add("act", lambda e, st=st, s4=s4, b=b: e.copy(st.v[:, s4, 0:TT], PSb[b]),
                              reads=PK(b), writes=st.ki(s4))
                    else:
                        S.add("dve", lambda e, st=st, s4=s4, b=b: e.tensor_copy(st.v[:, s4, TT:D], PSb[b]),
                              reads=PK(b), writes=st.ki(s4))
            S.add("sp", lambda e, i=i, st=st: e.dma_start(
                out=y_d[i * TT:(i + 1) * TT, :].rearrange("(s p) d -> p s d", p=P), in_=st.v),
                reads=st.k(), writes=[("Y", i)], dma=True)

        S.number()
        block = ctx.enter_context(nc.Block())

        @block.tensor
        def _(e):
            S.emit("pe", e, sems, dsems)

        @block.scalar
        def _(e):
            S.emit("act", e, sems, dsems)

        @block.vector
        def _(e):
            S.emit("dve", e, sems, dsems)

        @block.gpsimd
        def _(e):
            S.emit("pool", e, sems, dsems)

        @block.sync
        def _(e):
            S.emit("sp", e, sems, dsems, final=True)

    return nc


def _consts():
    cst = np.zeros((P, 6, P), np.float32)
    idx = np.arange(P)
    cst[idx, 0, idx] = 1.0
    d = idx % HD
    e = d % 32
    partner = np.where(e < 16, idx + 16, idx - 16)
    cst[partner, 1, idx] = 1.0
    cst[:, 2, :] = (idx[:, None] // HD == idx[None, :] // HD) / float(HD)
    cst[:, 3, :] = 1.0 / D
    cst[:, 4, :] = np.where(idx[:, None] >= idx[None, :], 0.0, -30000.0)
    cst[:, 5, :] = np.where(idx[:, None] <= idx[None, :], 0.0, -30000.0)
    tok = np.arange(NS)
    row = (tok // 64).astype(np.float32)
    col = (tok % 64).astype(np.float32)
    freqs = (1.0 / np.power(np.float32(10000.0), np.arange(16, dtype=np.float32) / np.float32(16))).astype(np.float32)
    f = e % 16
    pos = np.where((d // 32 == 0)[:, None], row[None, :], col[None, :]).astype(np.float32)
    ang = (pos * freqs[f][:, None]).astype(np.float32)
    rc = np.cos(ang).astype(np.float32)
    sn = np.sin(ang).astype(np.float32)
    rs = np.where((e < 16)[:, None], -sn, sn).astype(np.float32)
    return cst, rc, rs


_NC_CACHE = {}


def _prep(x_prompt, x_sample, cache_k, cache_v, c, c_ctx, w_mod, b_mod, norm_g, w_qkv, w_o,
          q_norm_g, k_norm_g, sink, w_ffn_in, w_ffn_out):
    f32 = np.float32
    x_prompt = np.asarray(x_prompt, f32)
    x_sample = np.asarray(x_sample, f32)
    cache_k = np.asarray(cache_k, f32)
    cache_v = np.asarray(cache_v, f32)
    c = np.asarray(c, f32)
    c_ctx = np.asarray(c_ctx, f32)
    w_qkv = np.asarray(w_qkv, f32)
    w_o = np.asarray(w_o, f32)
    hq = np.concatenate([[cc, 8 + cc] for cc in range(8)])
    perm_q = (hq[:, None] * HD + np.arange(HD)[None, :]).reshape(-1)
    kvo = np.array([0, 2, 1, 3])
    perm_k = (kvo[:, None] * HD + np.arange(HD)[None, :]).reshape(-1)
    cols = np.concatenate([perm_q, D + perm_k, D + 256 + np.arange(256)])
    w_qkv_p = np.ascontiguousarray(w_qkv[:, :, cols])
    w_o_p = np.ascontiguousarray(w_o[:, perm_q, :])
    ck_p = np.ascontiguousarray(cache_k[:, :, :, kvo, :]).reshape(N_CORES, DEPTH, CTX, 256)
    cv_p = np.ascontiguousarray(cache_v).reshape(N_CORES, DEPTH, CTX, 256)
    cst, rc, rs = _consts()
    shared = {
        "w_mod": np.ascontiguousarray(np.asarray(w_mod, f32)),
        "b_mod": np.ascontiguousarray(np.asarray(b_mod, f32)),
        "norm_g": np.ascontiguousarray(np.asarray(norm_g, f32).reshape(DEPTH * 3, D)),
        "w_qkv": w_qkv_p,
        "w_o": w_o_p,
        "qk_g": np.ascontiguousarray(np.concatenate([np.asarray(q_norm_g, f32), np.asarray(k_norm_g, f32)], 0)),
        "sink": np.ascontiguousarray(np.asarray(sink, f32).reshape(1, 32)),
        "w_in": np.ascontiguousarray(np.asarray(w_ffn_in, f32)),
        "w_out": np.ascontiguousarray(np.asarray(w_ffn_out, f32)),
        "cst": cst, "rope_c": rc, "rope_s": rs,
    }
    in_maps = []
    for i in range(N_CORES):
        m = dict(shared)
        m["x"] = np.ascontiguousarray(np.concatenate(
            [x_sample[i], x_prompt[4 * i:4 * i + 4].reshape(NPR, D)], 0))
        m["ck"] = ck_p[i]
        m["cv"] = cv_p[i]
        m["cond"] = np.ascontiguousarray(np.stack([c[i], c_ctx], 0))
        in_maps.append(m)
    return in_maps


def kernel(x_prompt, x_sample, cache_k, cache_v, c, c_ctx, w_mod, b_mod, norm_g, w_qkv, w_o,
           q_norm_g, k_norm_g, sink, w_ffn_in, w_ffn_out):
    f32 = np.float32
    in_maps = _prep(x_prompt, x_sample, cache_k, cache_v, c, c_ctx, w_mod, b_mod, norm_g, w_qkv, w_o,
                    q_norm_g, k_norm_g, sink, w_ffn_in, w_ffn_out)
    if "nc" not in _NC_CACHE:
        _NC_CACHE["nc"] = build()
    nc = _NC_CACHE["nc"]
    res = run_bass_kernel_spmd(nc, in_maps, core_ids=list(range(N_CORES)))
    y_prompt = np.empty((32, 256, D), f32)
    y_sample = np.empty((8, NS, D), f32)
    new_k = np.empty((32, DEPTH, 256, NKV, HD), f32)
    new_v = np.empty((32, DEPTH, 256, NKV, HD), f32)
    for i in range(N_CORES):
        r = res.results[i]
        y = np.asarray(r["y"])
        y_sample[i] = y[:NS]
        y_prompt[4 * i:4 * i + 4] = y[NS:].reshape(4, 256, D)
        new_k[4 * i:4 * i + 4] = np.asarray(r["nk"]).reshape(4, DEPTH, 256, NKV, HD)
        new_v[4 * i:4 * i + 4] = np.asarray(r["nv"]).reshape(4, DEPTH, 256, NKV, HD)
    return (y_prompt, y_sample, new_k, new_v)
```

```python
import numpy as np
from contextlib import ExitStack
import concourse.bass as bass
import concourse.mybir as mybir
from concourse.bass_utils import run_bass_kernel_spmd

F32 = mybir.dt.float32
BF16 = mybir.dt.bfloat16
U8 = mybir.dt.uint8
AF = mybir.ActivationFunctionType
ALU = mybir.AluOpType

P = 128
D = 1024
KC = 8
DFF = 2816
JC = 22
NH = 16
NKV = 4
HD = 64
TT = 512
NS = 4096
NPR = 1024
NTOK = NS + NPR
NT = NTOK // TT
CTX = 512
DEPTH = 4
QKVC = 1536
NKT = (CTX + NS) // P
EPS = 1e-6
GRAN = 256
NDQ = 8
N_CORES = 8

ENGS = ("pe", "act", "dve", "pool", "sp")


class Op:
    __slots__ = ("eng", "fn", "deps", "signal", "idx", "dma", "dseq")

    def __init__(self, eng, fn, dma):
        self.eng = eng
        self.fn = fn
        self.dma = dma
        self.deps = ()
        self.signal = False
        self.idx = 0
        self.dseq = -1


class Sched:
    def __init__(self):
        self.streams = {e: [] for e in ENGS}
        self.lastw = {}
        self.readers = {}
        self.dma_count = {e: 0 for e in ENGS}

    def add(self, eng, fn, reads=(), writes=(), dma=False):
        o = Op(eng, fn, dma)
        pr = [k for k in reads if k[0] == "P"]
        if pr:
            reads = [k for k in reads if k[0] != "P"]
            writes = list(writes) + pr
        raw = set()
        war = set()
        for k in reads:
            w = self.lastw.get(k)
            if w is not None:
                raw.add(w)
        for k in writes:
            w = self.lastw.get(k)
            if w is not None:
                raw.add(w)
            rd = self.readers.get(k)
            if rd:
                war.update(rd.values())
        real = []
        for d in raw:
            if d.dma or dma:
                real.append(d)
            elif d.eng == eng:
                if eng != "pe":
                    real.append(d)
            else:
                real.append(d)
        for d in war:
            if d in raw:
                continue
            if d.dma or dma:
                real.append(d)
            elif d.eng != eng:
                real.append(d)
        for d in real:
            d.signal = True
        o.deps = real
        for k in reads:
            rd = self.readers.get(k)
            if rd is None:
                rd = self.readers[k] = {}
            rd[("d", id(o)) if dma else eng] = o
        for k in writes:
            self.lastw[k] = o
            self.readers[k] = {}
        if dma:
            o.dseq = self.dma_count[eng]
            self.dma_count[eng] += 1
            o.signal = True
        self.streams[eng].append(o)
        return o

    def number(self):
        for e in ENGS:
            c = 0
            for o in self.streams[e]:
                if (not o.dma) and o.signal:
                    c += 1
                    o.idx = c

    def emit(self, eng, E, sems, dsems, final=False):
        waited = {}
        for o in self.streams[eng]:
            need = {}
            for d in o.deps:
                if d.dma:
                    s = dsems[d.eng][d.dseq % NDQ]
                    v = 16 * (d.dseq // NDQ + 1)
                else:
                    s = sems[d.eng]
                    v = d.idx
                cur = need.get(s.name)
                if cur is None or cur[1] < v:
                    need[s.name] = (s, v)
            if o.dma and o.dseq >= NDQ:
                s = dsems[eng][o.dseq % NDQ]
                v = 16 * (o.dseq // NDQ)
                cur = need.get(s.name)
                if cur is None or cur[1] < v:
                    need[s.name] = (s, v)
            for name, (s, v) in need.items():
                if waited.get(name, 0) < v:
                    E.wait_ge(s, v)
                    waited[name] = v
            ins = o.fn(E)
            if o.dma:
                ins.then_inc(dsems[eng][o.dseq % NDQ], 16)
            elif o.signal:
                ins.then_inc(sems[eng], 1)
        if final:
            for q in ENGS:
                n = self.dma_count[q]
                for i in range(min(NDQ, n)):
                    cnt = (n - i + NDQ - 1) // NDQ
                    if cnt > 0:
                        E.wait_ge(dsems[q][i], 16 * cnt)


class Buf:
    arena = None

    def __init__(self, off, shape, dt):
        self.off = off
        self.shape = tuple(shape)
        self.dt = dt
        self.esz = 4 if dt == F32 else 2
        n = 1
        for s in shape:
            n *= s
        self.n = n
        self.nbytes = n * self.esz
        self._ap = None

    @property
    def v(self):
        if self._ap is None:
            a = Buf.arena[:, self.off:self.off + self.nbytes].bitcast(self.dt)
            if len(self.shape) == 2:
                a = a.rearrange("p (a b) -> p a b", a=self.shape[0])
            elif len(self.shape) == 3:
                a = a.rearrange("p (a b c) -> p a b c", a=self.shape[0], b=self.shape[1])
            elif len(self.shape) == 4:
                a = a.rearrange("p (a b c d) -> p a b c d", a=self.shape[0], b=self.shape[1], c=self.shape[2])
            self._ap = a
        return self._ap

    def k(self, lo=0, hi=None):
        if hi is None:
            hi = self.n
        b0 = (self.off + lo * self.esz) // GRAN
        b1 = (self.off + hi * self.esz - 1) // GRAN
        return [("A", g) for g in range(b0, b1 + 1)]

    def ki(self, *idx):
        stride = self.n
        lo = 0
        for d, i in enumerate(idx):
            stride //= self.shape[d]
            lo += i * stride
        return self.k(lo, lo + stride)


class Alloc:
    def __init__(self):
        self.top = 0
        self.peak = 0

    def __call__(self, shape, dt, align=1024):
        off = (self.top + align - 1) // align * align
        b = Buf(off, shape, dt)
        self.top = off + b.nbytes
        self.peak = max(self.peak, self.top)
        return b


def PK(*banks):
    return [("P", b) for b in banks]


def build(depth=DEPTH, dbg=False, phases=("f1", "att", "f2"), mods=True):
    nc = bass.Bass("TRN2", target_bir_lowering=False)
    S = Sched()

    def din(name, shape):
        return nc.dram_tensor(name, list(shape), F32, kind="ExternalInput").ap()

    def dout(name, shape):
        return nc.dram_tensor(name, list(shape), F32, kind="ExternalOutput").ap()

    x_d = din("x", [NTOK, D])
    ck_d = din("ck", [DEPTH, CTX, 256])
    cv_d = din("cv", [DEPTH, CTX, 256])
    cond_d = din("cond", [2, D])
    wmod_d = din("w_mod", [DEPTH, D, 9 * D])
    bmod_d = din("b_mod", [DEPTH, 9 * D])
    ng_d = din("norm_g", [DEPTH * 3, D])
    wqkv_d = din("w_qkv", [DEPTH, D, QKVC])
    wo_d = din("w_o", [DEPTH, D, D])
    qkg_d = din("qk_g", [8, HD])
    sink_d = din("sink", [1, 32])
    win_d = din("w_in", [DEPTH, 2, D, 2 * DFF])
    wout_d = din("w_out", [DEPTH, 2, DFF, D])
    cst_d = din("cst", [P, 6, P])
    rc_d = din("rope_c", [P, NS])
    rs_d = din("rope_s", [P, NS])
    y_d = dout("y", [NTOK, D])
    nk_d = dout("nk", [4, DEPTH, 256, 256])
    nv_d = dout("nv", [4, DEPTH, 256, 256])
    xT_d = nc.dram_tensor("xT_scr", [D, NTOK], F32).ap()
    xT_v = xT_d.rearrange("(k p) t -> p k t", p=P)

    A = Alloc()
    xt = [A([KC, TT], F32) for _ in range(2)]
    xn = A([KC, TT], BF16)
    MODV = A([DEPTH * 2 * 3 * 3, KC], F32)
    identf = A([P], F32)
    cb = A([6, P], BF16)
    epsT = A([1], F32, align=64)
    ones2 = A([2], BF16, align=64)
    es = A([32], F32, align=64)
    qkg = A([8], F32, align=64)
    sd = A([TT], F32)
    rstd = sd
    tring = [A([TT], F32) for _ in range(2)]
    sqring = [A([TT], BF16) for _ in range(3)]
    base = A.top
    Win = A([KC, 2 * DFF], BF16)
    Wout = A([JC, D], BF16)
    hT = A([JC // 2, TT], BF16)
    sg = [A([TT], F32) for _ in range(2)]
    ffn_top = A.top
    A.top = base
    KT = A([2, CTX + NS], BF16)
    VA = A([NKT, NKV, P], BF16)
    Wqkv = A([KC, QKVC], BF16)
    Wo = A([KC, D], BF16)
    QT = A([KC, TT], BF16)
    AT = A([KC, TT], BF16)
    PT = [A([2, TT], BF16) for _ in range(3)]
    osb = [A([TT], F32) for _ in range(2)]
    ropeC = [A([TT], F32) for _ in range(2)]
    ropeS = [A([TT], F32) for _ in range(2)]
    zr = [A([TT], F32) for _ in range(2)]
    zb = [A([TT], BF16) for _ in range(2)]
    ra_off = A.top
    ra = [A([TT], F32) for _ in range(2)]
    rb_off = A.top
    rb = [A([TT], F32) for _ in range(2)]
    sdq = [A([TT], F32) for _ in range(2)]
    rq = sdq
    rec = [A([TT], F32) for _ in range(2)]
    rs_ = rec
    ckst = Buf(osb[0].off, [4, 256], BF16)
    vst = Buf(rb_off, [4, 256], F32)
    nkst = Buf(ra_off, [4, 256], F32)
    att_top = A.top
    A.top = base
    stage = [A([4, D], F32) for _ in range(2)]
    wm = [A([KC, D], BF16) for _ in range(2)]
    bm = [A([D], BF16) for _ in range(2)]
    ctok = A([D], F32)
    ngtok = A([D], F32)
    qktok = A([P], F32)
    scb = A([KC, 2], BF16, align=64)
    ngT = A([KC, 12], F32, align=64)
    modraw = A([72, 2], F32, align=64)
    estmp = A([32], F32, align=64)
    qktmp = A([8], F32, align=64)
    setup_top = A.top
    total = max(ffn_top, att_top, setup_top)
    total = (total + 1023) // 1024 * 1024
    assert total <= 212000, total

    with ExitStack() as ctx:
        arena_t = ctx.enter_context(nc.sbuf_tensor("arena", [P, total], U8))
        ps_t = ctx.enter_context(nc.psum_tensor("ps", [P, 8 * TT], F32))
        sems = {e: ctx.enter_context(nc.semaphore("s_" + e)) for e in ENGS}
        dsems = {e: [ctx.enter_context(nc.semaphore("d_%s%d" % (e, i))) for i in range(NDQ)]
                 for e in ("sp", "pool")}
        for e in ENGS:
            dsems.setdefault(e, dsems["sp"])
        Buf.arena = arena_t[:, :]
        PSb = [ps_t[:, b * TT:(b + 1) * TT] for b in range(8)]

        def PSv(b0, nb):
            return ps_t[:, b0 * TT:(b0 + nb) * TT]

        def mod(l, j, s, w):
            r = ((l * 2 + j) * 3 + s) * 3 + w
            return MODV.v[:, r, :]

        ident_f = identf.v
        ident_b = cb.v[:, 0, :]
        perm_b = cb.v[:, 1, :]
        blk_b = cb.v[:, 2, :]
        ones_b = cb.v[:, 3, :]
        mL = cb.v[:, 4, :]
        mR = cb.v[:, 5, :]

        S.add("sp", lambda e: e.dma_start(out=identf.v, in_=cst_d[:, 0, :]), writes=identf.k(), dma=True)
        S.add("pool", lambda e: e.dma_start(out=cb.v, in_=cst_d), writes=cb.k(), dma=True)
        S.add("pool", lambda e: e.memset(epsT.v, EPS), writes=epsT.k())
        S.add("pool", lambda e: e.memset(ones2.v, 1.0), writes=ones2.k())
        S.add("sp", lambda e: e.dma_start(out=ctok.v[0:2, :], in_=cond_d), writes=ctok.k(), dma=True)
        S.add("sp", lambda e: e.dma_start(out=ngtok.v[0:12, :], in_=ng_d), writes=ngtok.k(), dma=True)
        for r0, c0 in ((0, 0), (0, 64)):
            S.add("sp", lambda e, r0=r0, c0=c0: e.dma_start(out=qktok.v[0:8, c0:c0 + 64], in_=qkg_d),
                  writes=qktok.k(), dma=True)
        S.add("sp", lambda e: e.dma_start(out=estmp.v, in_=sink_d.partition_broadcast(P)),
              writes=estmp.k(), dma=True)
        S.add("act", lambda e: e.activation(out=es.v, in_=estmp.v, func=AF.Exp), reads=estmp.k(), writes=es.k())
        for k in range(KC):
            S.add("pe", lambda e, k=k: e.transpose(PSb[1][:, k * 2:(k + 1) * 2], ctok.v[0:2, k * P:(k + 1) * P],
                                                    ident_f[0:2, 0:2]),
                  reads=ctok.k() + identf.k(), writes=PK(1))
        S.add("act", lambda e: e.activation(out=scb.v, in_=PSb[1][:, 0:16].rearrange("p (k j) -> p k j", j=2),
                                            func=AF.Silu), reads=PK(1), writes=scb.k())
        for k in range(KC):
            S.add("pe", lambda e, k=k: e.transpose(PSb[2][:, k * 12:(k + 1) * 12], ngtok.v[0:12, k * P:(k + 1) * P],
                                                    ident_f[0:12, 0:12]),
                  reads=ngtok.k() + identf.k(), writes=PK(2))
        S.add("dve", lambda e: e.tensor_copy(ngT.v, PSb[2][:, 0:96].rearrange("p (k j) -> p k j", j=12)),
              reads=PK(2), writes=ngT.k())
        S.add("pe", lambda e: e.transpose(PSb[3][:, 0:8], qktok.v[0:8, :], ident_f[0:8, 0:8]),
              reads=qktok.k() + identf.k(), writes=PK(3))
        S.add("dve", lambda e: e.tensor_copy(qktmp.v, PSb[3][:, 0:8]), reads=PK(3), writes=qktmp.k())
        S.add("dve", lambda e: e.tensor_scalar(out=qkg.v[:, 0:4], in0=qktmp.v[:, 0:4], scalar1=HD ** -0.5,
                                               scalar2=None, op0=ALU.mult), reads=qktmp.k(), writes=qkg.k())
        S.add("dve", lambda e: e.tensor_copy(qkg.v[:, 4:8], qktmp.v[:, 4:8]), reads=qktmp.k(), writes=qkg.k())

        def t0_tile(i):
            st = stage[i % 2]
            sl = xt[i % 2]
            S.add("sp", lambda e, i=i, st=st: e.dma_start(
                out=st.v, in_=x_d[i * TT:(i + 1) * TT, :].rearrange("(s p) d -> p s d", p=P)),
                writes=st.k(), dma=True)
            for k in range(KC):
                b = k % 4
                for s in range(4):
                    S.add("pe", lambda e, st=st, k=k, s=s, b=b: e.transpose(
                        PSb[b][:, s * P:(s + 1) * P], st.v[:, s, k * P:(k + 1) * P], ident_f),
                        reads=st.ki(s) + identf.k(), writes=PK(b))
                if k % 2 == 0:
                    S.add("act", lambda e, sl=sl, k=k, b=b: e.copy(sl.v[:, k, :], PSb[b]),
                          reads=PK(b), writes=sl.ki(k))
                else:
                    S.add("dve", lambda e, sl=sl, k=k, b=b: e.tensor_copy(sl.v[:, k, :], PSb[b]),
                          reads=PK(b), writes=sl.ki(k))
            S.add("sp", lambda e, i=i, sl=sl: e.dma_start(out=xT_v[:, :, i * TT:(i + 1) * TT], in_=sl.v),
                  reads=sl.k(), writes=[("X", i)], dma=True)


        def mod_block(l, v):
            mb = 4 + l % 2
            w_ = wm[v % 2]
            b_ = bm[v % 2]
            S.add("pool", lambda e, l=l, v=v, w_=w_: e.dma_start(
                out=w_.v, in_=wmod_d[l, :, v * D:(v + 1) * D].rearrange("(k p) c -> p k c", p=P)),
                writes=w_.k(), dma=True)
            S.add("pool", lambda e, l=l, v=v, b_=b_: e.dma_start(
                out=b_.v[0:1, :], in_=bmod_d[l:l + 1, v * D:(v + 1) * D]),
                writes=b_.k(), dma=True)
            for m in range(KC):
                c = v * KC + m
                for k in range(KC):
                    S.add("pe", lambda e, w_=w_, m=m, k=k, c=c, mb=mb: e.matmul(
                        PSb[mb][:, c * 2:c * 2 + 2], w_.v[:, k, m * P:(m + 1) * P], scb.v[:, k, :],
                        start=(c == 0 and k == 0), stop=False, skip_group_check=True),
                        reads=w_.ki(k) + scb.k(), writes=PK(mb))
                S.add("pe", lambda e, b_=b_, m=m, c=c, mb=mb: e.matmul(
                    PSb[mb][:, c * 2:c * 2 + 2], b_.v[0:1, m * P:(m + 1) * P], ones2.v[0:1, :],
                    start=False, stop=True, skip_group_check=True),
                    reads=b_.k() + ones2.k(), writes=PK(mb))

        def mod_final(l):
            mb = 4 + l % 2
            S.add("dve", lambda e, mb=mb: e.tensor_copy(modraw.v, PSb[mb][:, 0:144].rearrange("p (c j) -> p c j", j=2)),
                  reads=PK(mb), writes=modraw.k())
            for j in range(2):
                for s in range(3):
                    S.add("dve", lambda e, l=l, j=j, s=s: e.scalar_tensor_tensor(
                        out=mod(l, j, s, 0), in0=modraw.v[:, (3 * s + 1) * 8:(3 * s + 2) * 8, j], scalar=1.0,
                        in1=ngT.v[:, :, l * 3 + s], op0=ALU.add, op1=ALU.mult),
                        reads=modraw.k() + ngT.k(), writes=MODV.k())
                    S.add("dve", lambda e, l=l, j=j, s=s: e.tensor_copy(
                        mod(l, j, s, 1), modraw.v[:, (3 * s) * 8:(3 * s + 1) * 8, j]),
                        reads=modraw.k(), writes=MODV.k())
                    S.add("dve", lambda e, l=l, j=j, s=s: e.tensor_scalar(
                        out=mod(l, j, s, 2), in0=modraw.v[:, (3 * s + 2) * 8:(3 * s + 3) * 8, j],
                        scalar1=(1.0 if s == 1 else 0.5), scalar2=None, op0=ALU.mult),
                        reads=modraw.k(), writes=MODV.k())


        mod_work = []
        for l in range(depth if mods else 0):
            for v in range(9):
                mod_work.append(lambda l=l, v=v: mod_block(l, v))
            mod_work.append(lambda l=l: mod_final(l))
        per = (len(mod_work) + NT - 1) // NT
        for i in range(NT):
            t0_tile(i)
            for fn in mod_work[i * per:(i + 1) * per]:
                fn()

        def load_x(i):
            sl = xt[i % 2]
            S.add("sp", lambda e, i=i, sl=sl: e.dma_start(out=sl.v, in_=xT_v[:, :, i * TT:(i + 1) * TT]),
                  reads=[("X", i)], writes=sl.k(), dma=True)

        def store_x(i):
            sl = xt[i % 2]
            S.add("sp", lambda e, i=i, sl=sl: e.dma_start(out=xT_v[:, :, i * TT:(i + 1) * TT], in_=sl.v),
                  reads=sl.k(), writes=[("X", i)], dma=True)

        cnt = {"sq": 0, "t": 0}

        def norm_steps(i, l, s, msb=6, lean=False):
            j = 0 if i < NS // TT else 1
            sl = xt[i % 2]

            def st_sq(k):
                q = sqring[cnt["sq"] % 3]
                cnt["sq"] += 1
                if lean:
                    S.add("dve", lambda e: e.tensor_tensor(out=q.v, in0=sl.v[:, k, :], in1=sl.v[:, k, :], op=ALU.mult),
                          reads=sl.ki(k), writes=q.k())
                else:
                    S.add("act", lambda e: e.activation(out=q.v, in_=sl.v[:, k, :], func=AF.Square),
                          reads=sl.ki(k), writes=q.k())
                S.add("pe", lambda e: e.matmul(PSb[msb], ones_b, q.v, start=(k == 0), stop=(k == KC - 1)),
                      reads=q.k() + cb.k(), writes=PK(msb))

            def st_rs():
                S.add("act", lambda e: e.activation(out=sd.v, in_=PSb[msb], func=AF.Ln, bias=epsT.v, scale=1.0),
                      reads=PK(msb) + epsT.k(), writes=sd.k())
                S.add("act", lambda e: e.activation(out=sd.v, in_=sd.v, func=AF.Exp, scale=-0.5),
                      reads=sd.k(), writes=sd.k())

            def st_xn():
                for k in range(KC):
                    t = tring[cnt["t"] % 2]
                    cnt["t"] += 1
                    S.add("dve", lambda e, k=k, t=t: e.tensor_tensor(out=t.v, in0=sl.v[:, k, :], in1=rstd.v, op=ALU.mult),
                          reads=sl.ki(k) + rstd.k(), writes=t.k())
                    if k % 2 == 0 or lean:
                        S.add("pool", lambda e, k=k, t=t: e.tensor_scalar(
                            out=xn.v[:, k, :], in0=t.v, scalar1=mod(l, j, s, 0)[:, k:k + 1], scalar2=mod(l, j, s, 1)[:, k:k + 1],
                            op0=ALU.mult, op1=ALU.add), reads=t.k() + MODV.k(), writes=xn.ki(k))
                    else:
                        S.add("act", lambda e, k=k, t=t: e.activation(
                            out=xn.v[:, k, :], in_=t.v, func=AF.Identity, scale=mod(l, j, s, 0)[:, k:k + 1],
                            bias=mod(l, j, s, 1)[:, k:k + 1]), reads=t.k() + MODV.k(), writes=xn.ki(k))

            return [(lambda k=k: st_sq(k)) for k in range(KC)] + [st_rs, st_xn]

        def norm_tile(i, l, s, msb=6):
            for st in norm_steps(i, l, s, msb):
                st()

        def residual(i, l, s, m, bank):
            j = 0 if i < NS // TT else 1
            sl = xt[i % 2]
            S.add("dve", lambda e, sl=sl, m=m, bank=bank, l=l, j=j, s=s: e.scalar_tensor_tensor(
                out=sl.v[:, m, :], in0=PSb[bank], scalar=mod(l, j, s, 2)[:, m:m + 1], in1=sl.v[:, m, :],
                op0=ALU.mult, op1=ALU.add), reads=PK(bank) + sl.ki(m) + MODV.k(), writes=sl.ki(m))

        def ffn_pass(l, f):
            s = 0 if f == 0 else 2
            HC = DFF // 2
            for hf in range(2):
                for gu_ in range(2):
                    for k in range(KC):
                        c0 = gu_ * DFF + hf * HC
                        S.add("pool", lambda e, k=k, c0=c0: e.dma_start(
                            out=Win.v[:, k, c0:c0 + HC], in_=win_d[l, f, k * P:(k + 1) * P, c0:c0 + HC]),
                            writes=Win.k(k * 2 * DFF + c0, k * 2 * DFF + c0 + HC), dma=True)
                j0 = hf * 11
                S.add("pool", lambda e, j0=j0: e.dma_start(
                    out=Wout.v[:, j0:j0 + 11, :],
                    in_=wout_d[l, f, j0 * P:(j0 + 11) * P, :].rearrange("(j p) n -> p j n", p=P)),
                    writes=Wout.k(j0 * D, (j0 + 11) * D), dma=True)

            HJ = JC // 2

            def gu(i, hf, inter=()):
                for jj in range(HJ):
                    if jj < len(inter):
                        inter[jj]()
                    jc = hf * HJ + jj
                    gb = (jc % 2) * 2
                    ub = gb + 1
                    for (bank, c0) in ((gb, jc * P), (ub, DFF + jc * P)):
                        for k in range(KC):
                            S.add("pe", lambda e, bank=bank, c0=c0, k=k: e.matmul(
                                PSb[bank], Win.v[:, k, c0:c0 + P], xn.v[:, k, :], start=(k == 0), stop=(k == KC - 1)),
                                reads=Win.k(k * 2 * DFF + c0, k * 2 * DFF + c0 + P) + xn.ki(k), writes=PK(bank))
                    g_ = sg[jc % 2]
                    S.add("act", lambda e, g_=g_, gb=gb: e.activation(out=g_.v, in_=PSb[gb], func=AF.Silu),
                          reads=PK(gb), writes=g_.k())
                    S.add("dve", lambda e, g_=g_, ub=ub, jj=jj: e.tensor_tensor(
                        out=hT.v[:, jj, :], in0=PSb[ub], in1=g_.v, op=ALU.mult),
                        reads=PK(ub) + g_.k(), writes=hT.ki(jj))

            def wout(i, hf):
                for m in range(KC):
                    bank = 4 + m % 2
                    for jj in range(HJ):
                        jc = hf * HJ + jj
                        S.add("pe", lambda e, bank=bank, jc=jc, jj=jj, m=m: e.matmul(
                            PSb[bank], Wout.v[:, jc, m * P:(m + 1) * P], hT.v[:, jj, :],
                            start=(jj == 0), stop=(jj == HJ - 1)),
                            reads=Wout.k(jc * D + m * P, jc * D + (m + 1) * P) + hT.ki(jj), writes=PK(bank))
                    residual(i, l, s, m, bank)

            load_x(0)
            norm_tile(0, l, s)
            for i in range(NT):
                nxt = []
                if i + 1 < NT:
                    load_x(i + 1)
                    nxt = norm_steps(i + 1, l, s)
                gu(i, 0)
                wout(i, 0)
                gu(i, 1, nxt[:KC])
                for st in nxt[KC:]:
                    st()
                wout(i, 1)
                store_x(i)

        def att_weights(l):
            S.add("pool", lambda e: e.dma_start(out=Wqkv.v, in_=wqkv_d[l].rearrange("(k p) c -> p k c", p=P)),
                  writes=Wqkv.k(), dma=True)
            S.add("pool", lambda e: e.dma_start(out=Wo.v, in_=wo_d[l].rearrange("(c p) n -> p c n", p=P)),
                  writes=Wo.k(), dma=True)
            S.add("pool", lambda e: e.memset(VA.v[:, :, :, HD:P], 1.0), writes=VA.k())

        def load_ctx(l):
            for t in range(4):
                S.add("pool", lambda e, t=t: e.dma_start(
                    out=VA.v[:, t, :, 0:HD],
                    in_=cv_d[l, t * P:(t + 1) * P, :].rearrange("p (g d) -> p g d", g=NKV)),
                    writes=VA.ki(t), dma=True)
            S.add("pool", lambda e: e.dma_start(out=ckst.v, in_=ck_d[l].rearrange("(t p) c -> p t c", p=P)),
                  writes=ckst.k(), dma=True)
            for t in range(4):
                for i2 in range(2):
                    S.add("pe", lambda e, t=t, i2=i2: e.transpose(
                        PSb[i2].bitcast(BF16)[:, t * P:(t + 1) * P], ckst.v[:, t, i2 * P:(i2 + 1) * P], ident_b),
                        reads=ckst.ki(t) + cb.k(), writes=PK(i2))
            for i2 in range(2):
                S.add("dve", lambda e, i2=i2: e.tensor_copy(KT.v[:, i2, 0:CTX], PSb[i2].bitcast(BF16)[:, 0:CTX]),
                      reads=PK(i2), writes=KT.k(i2 * (CTX + NS), i2 * (CTX + NS) + CTX))

        rcnt = {"c": 0, "r": 0}
        BX_, BY_ = 6, 7

        def load_rope(tok0):
            r = rcnt["r"] % 2
            rcnt["r"] += 1
            S.add("sp", lambda e: e.dma_start(out=ropeC[r].v, in_=rc_d[:, tok0:tok0 + TT]),
                  writes=ropeC[r].k(), dma=True)
            S.add("sp", lambda e: e.dma_start(out=ropeS[r].v, in_=rs_d[:, tok0:tok0 + TT]),
                  writes=ropeS[r].k(), dma=True)
            return r

        def qk_steps(l, wcol, gcol, rope, dst_ap, dst_keys, banks=(6, 7)):
            c = rcnt["c"]
            rcnt["c"] += 1
            r2 = c % 2
            z = zr[r2]
            q = sqring[c % 3]
            BX, BY = banks
            rr = rope

            def s_proj():
                for k in range(KC):
                    S.add("pe", lambda e, k=k: e.matmul(PSb[BX], Wqkv.v[:, k, wcol:wcol + P], xn.v[:, k, :],
                                                        start=(k == 0), stop=(k == KC - 1)),
                          reads=Wqkv.k(k * QKVC + wcol, k * QKVC + wcol + P) + xn.ki(k), writes=PK(BX))

            def s_sq():
                S.add("act", lambda e: e.activation(out=q.v, in_=PSb[BX], func=AF.Square),
                      reads=PK(BX), writes=q.k())

            def s_ms():
                S.add("pe", lambda e: e.matmul(PSb[BY], blk_b, q.v, start=True, stop=True),
                      reads=q.k() + cb.k(), writes=PK(BY))

            def s_rs():
                S.add("act", lambda e: e.activation(out=sdq[r2].v, in_=PSb[BY], func=AF.Ln, bias=epsT.v, scale=1.0),
                      reads=PK(BY) + epsT.k(), writes=sdq[r2].k())
                S.add("act", lambda e: e.activation(out=sdq[r2].v, in_=sdq[r2].v, func=AF.Exp, scale=-0.5),
                      reads=sdq[r2].k(), writes=sdq[r2].k())

            def s_z():
                S.add("dve", lambda e: e.scalar_tensor_tensor(
                    out=z.v, in0=PSb[BX], scalar=qkg.v[:, gcol:gcol + 1], in1=sdq[r2].v, op0=ALU.mult, op1=ALU.mult),
                    reads=PK(BX) + sdq[r2].k() + qkg.k(), writes=z.k())

            if rr is None:
                def s_out():
                    S.add("act", lambda e: e.copy(dst_ap, z.v), reads=z.k(), writes=dst_keys)
                return [s_proj, s_sq, s_ms, s_rs, s_z, s_out], z

            def s_zb():
                S.add("dve", lambda e: e.tensor_copy(zb[r2].v, z.v), reads=z.k(), writes=zb[r2].k())

            def s_perm():
                S.add("pe", lambda e: e.matmul(PSb[BY], perm_b, zb[r2].v, start=True, stop=True),
                      reads=zb[r2].k() + cb.k(), writes=PK(BY))

            def s_rope():
                S.add("pool", lambda e: e.tensor_tensor(out=ra[r2].v, in0=z.v, in1=ropeC[rr].v, op=ALU.mult),
                      reads=z.k() + ropeC[rr].k(), writes=ra[r2].k())
                S.add("dve", lambda e: e.tensor_tensor(out=rb[r2].v, in0=PSb[BY], in1=ropeS[rr].v, op=ALU.mult),
                      reads=PK(BY) + ropeS[rr].k(), writes=rb[r2].k())
                S.add("dve", lambda e: e.tensor_tensor(out=dst_ap, in0=ra[r2].v, in1=rb[r2].v, op=ALU.add),
                      reads=ra[r2].k() + rb[r2].k(), writes=dst_keys)

            return [s_proj, s_sq, s_ms, s_rs, s_z, s_zb, s_perm, s_rope], z

        def run(steps):
            for st in steps:
                st()

        def run_interleaved(lists, skew=2):
            T = max(len(x) + c * skew for c, x in enumerate(lists))
            for t in range(T):
                for c, x in enumerate(lists):
                    idx = t - c * skew
                    if 0 <= idx < len(x):
                        x[idx]()

        BANKPAIRS = ((6, 7), (0, 1), (2, 3))

        def kv_tile(l, key0, kt0, rope, out_seq0):
            lists = []
            for i2 in range(2):
                lo = i2 * (CTX + NS) + key0
                steps, z = qk_steps(l, D + i2 * P, 4 + l, rope, KT.v[:, i2, key0:key0 + TT], KT.k(lo, lo + TT),
                                    banks=BANKPAIRS[i2])
                if out_seq0 is not None:
                    def s_nk(i2=i2, z=z):
                        tb = 2 + i2
                        for s4 in range(4):
                            S.add("pe", lambda e, s4=s4: e.transpose(
                                PSb[tb][:, s4 * P:(s4 + 1) * P], z.v[:, s4 * P:(s4 + 1) * P], ident_f),
                                reads=z.k() + identf.k(), writes=PK(tb))
                        for gg in range(2):
                            g = i2 + 2 * gg
                            S.add("dve", lambda e, g=g, gg=gg: e.tensor_copy(
                                nkst.v[:, :, g * HD:(g + 1) * HD],
                                PSb[tb].rearrange("p (s c) -> p s c", s=4)[:, :, gg * HD:(gg + 1) * HD]),
                                reads=PK(tb), writes=nkst.k())
                    steps = steps + [s_nk]
                lists.append(steps)

            def s_v():
                for s4 in range(4):
                    vb = 4 + s4 // 2
                    c0 = (s4 % 2) * 256
                    for k in range(KC):
                        S.add("pe", lambda e, s4=s4, k=k, vb=vb, c0=c0: e.matmul(
                            PSb[vb][:, c0:c0 + 256], xn.v[:, k, s4 * P:(s4 + 1) * P], Wqkv.v[:, k, D + 256:QKVC],
                            start=(k == 0), stop=(k == KC - 1), skip_group_check=True),
                            reads=xn.ki(k) + Wqkv.k(k * QKVC + D + 256, (k + 1) * QKVC), writes=PK(vb))
                    S.add("act", lambda e, s4=s4, vb=vb, c0=c0: e.copy(
                        VA.v[:, kt0 + s4, :, 0:HD], PSb[vb][:, c0:c0 + 256].rearrange("p (g d) -> p g d", g=NKV)),
                        reads=PK(vb), writes=VA.ki(kt0 + s4))
                    if out_seq0 is not None:
                        S.add("dve", lambda e, s4=s4, vb=vb, c0=c0: e.tensor_copy(vst.v[:, s4, :], PSb[vb][:, c0:c0 + 256]),
                              reads=PK(vb), writes=vst.ki(s4))
            lists.append([s_v])
            run_interleaved(lists, skew=1)
            if out_seq0 is not None:
                for q2 in range(2):
                    S.add("sp", lambda e, q2=q2: e.dma_start(
                        out=nk_d[out_seq0 + q2, l].rearrange("(h p) c -> p h c", p=P), in_=nkst.v[:, 2 * q2:2 * q2 + 2, :]),
                        reads=nkst.k(), writes=[("NK", out_seq0 + q2, l)], dma=True)
                    S.add("sp", lambda e, q2=q2: e.dma_start(
                        out=nv_d[out_seq0 + q2, l].rearrange("(h p) c -> p h c", p=P), in_=vst.v[:, 2 * q2:2 * q2 + 2, :]),
                        reads=vst.k(), writes=[("NV", out_seq0 + q2, l)], dma=True)

        def q_chunk_steps(l, c, rope, banks=(6, 7)):
            return qk_steps(l, c * P, l, rope, QT.v[:, c, :], QT.ki(c), banks=banks)[0]

        def q_tile_free(l, rope):
            run_interleaved([q_chunk_steps(l, c, rope, BANKPAIRS[c % 3]) for c in range(KC)], skew=2)

        acnt = {"n": 0}

        def attend(l, jobs, use_sink, side=None, act_norm=False):
            items = [(c, jb, n == 0, n == len(jobs) - 1) for c in range(KC) for n, jb in enumerate(jobs)]
            slot = []
            pslot = []
            for _ in items:
                slot.append(acnt["n"] % 2)
                pslot.append(acnt["n"] % 3)
                acnt["n"] += 1

            def s_mm(a):
                c, (kt, key0, qlo, qhi, masks), first, last = items[a]
                g = slot[a]
                i2 = c // 4
                kk0 = i2 * (CTX + NS) + key0
                for h2 in range(2):
                    pr = slice(h2 * HD, (h2 + 1) * HD)
                    bank = 2 * g + h2
                    S.add("pe", lambda e, pr=pr, bank=bank: e.matmul(
                        PSb[bank][:, qlo:qhi], KT.v[pr, i2, key0:key0 + P], QT.v[pr, c, qlo:qhi],
                        start=True, stop=(not masks), skip_group_check=True),
                        reads=KT.k(kk0, kk0 + P) + QT.ki(c), writes=PK(bank))
                    for mi, (blk, side_) in enumerate(masks):
                        mk = mL if side_ == "L" else mR
                        S.add("pe", lambda e, bank=bank, blk=blk, mk=mk, mi=mi: e.matmul(
                            PSb[bank][:, blk * P:(blk + 1) * P], ident_b, mk,
                            start=False, stop=(mi == len(masks) - 1), skip_group_check=True),
                            reads=cb.k(), writes=PK(bank))

            pending = []

            def do_exp(a):
                c, (kt, key0, qlo, qhi, masks), first, last = items[a]
                g = slot[a]
                pt = PT[pslot[a]]
                S.add("act", lambda e: e.activation(
                    out=pt.v[:, :, qlo:qhi], in_=PSv(2 * g, 2).rearrange("p (h q) -> p h q", h=2)[:, :, qlo:qhi],
                    func=AF.Exp), reads=PK(2 * g, 2 * g + 1), writes=pt.k())

            def do_pv(a):
                c, (kt, key0, qlo, qhi, masks), first, last = items[a]
                pt = PT[pslot[a]]
                for h2 in range(2):
                    gk = c // 4 + 2 * h2
                    ob = 4 + h2
                    S.add("pe", lambda e, h2=h2, gk=gk, ob=ob: e.matmul(
                        PSb[ob][:, qlo:qhi], VA.v[:, kt, gk, :], pt.v[:, h2, qlo:qhi],
                        start=first, stop=last, skip_group_check=True),
                        reads=VA.ki(kt, gk) + pt.k(), writes=PK(ob))
                if last:
                    for h2 in range(2):
                        S.add("dve", lambda e, h2=h2: e.tensor_copy(osb[h2].v, PSb[4 + h2]),
                              reads=PK(4 + h2), writes=osb[h2].k())

                    def normalise(c=c):
                        for h2 in range(2):
                            o_ = osb[h2]
                            rc_ = rec[h2]
                            col = (l // 2) * NH + c + 8 * h2
                            if use_sink or act_norm:
                                if use_sink:
                                    S.add("act", lambda e, o_=o_, col=col: e.activation(
                                        out=o_.v[HD:P, :], in_=o_.v[HD:P, :], func=AF.Ln, bias=es.v[HD:P, col:col + 1], scale=1.0),
                                        reads=o_.k() + es.k(), writes=o_.k())
                                else:
                                    S.add("act", lambda e, o_=o_: e.activation(
                                        out=o_.v[HD:P, :], in_=o_.v[HD:P, :], func=AF.Ln),
                                        reads=o_.k(), writes=o_.k())
                                S.add("act", lambda e, o_=o_, rc_=rc_: e.activation(
                                    out=rc_.v[0:HD, :], in_=o_.v[HD:P, :], func=AF.Exp, scale=-1.0),
                                    reads=o_.k(), writes=rc_.k())
                            else:
                                S.add("dve", lambda e, o_=o_, rc_=rc_: e.reciprocal(rc_.v[0:HD, :], o_.v[HD:P, :]),
                                      reads=o_.k(), writes=rc_.k())
                            S.add("dve", lambda e, o_=o_, rc_=rc_, h2=h2: e.tensor_tensor(
                                out=AT.v[h2 * HD:(h2 + 1) * HD, c, :], in0=o_.v[0:HD, :], in1=rc_.v[0:HD, :], op=ALU.mult),
                                reads=o_.k() + rc_.k(), writes=AT.ki(c))
                    pending.append((a + 3, normalise))

            n = len(items)
            for a in range(min(2, n)):
                s_mm(a)
            for a in range(n):
                do_exp(a)
                if a + 2 < n:
                    s_mm(a + 2)
                do_pv(a)
                while pending and pending[0][0] <= a:
                    pending.pop(0)[1]()
                if side and a in side:
                    for fn in side[a]:
                        fn()
            while pending:
                pending.pop(0)[1]()

        def oproj_m(i, l, m):
            bank = BX_ + m % 2
            for c in range(KC):
                S.add("pe", lambda e, c=c: e.matmul(
                    PSb[bank], Wo.v[:, c, m * P:(m + 1) * P], AT.v[:, c, :], start=(c == 0), stop=(c == KC - 1)),
                    reads=Wo.k(c * D + m * P, c * D + (m + 1) * P) + AT.ki(c), writes=PK(bank))
            residual(i, l, 1, m, bank)

        def oproj_tile(i, l):
            for m in range(KC):
                oproj_m(i, l, m)

        def full_jobs(nkt):
            return [(kt, kt * P, 0, TT, []) for kt in range(nkt)]

        def window_jobs(ti):
            jobs = [(kt, kt * P, 0, TT, []) for kt in range(4)]
            for R in range(ti * 4 - 1, ti * 4 + 5):
                if R < 0 or R >= NS // P:
                    continue
                rr = R - ti * 4
                blo, bhi = max(0, rr - 1), min(3, rr + 1)
                masks = []
                for b in range(blo, bhi + 1):
                    if rr == b - 1:
                        masks.append((b, "L"))
                    elif rr == b + 1:
                        masks.append((b, "R"))
                jobs.append((4 + R, CTX + R * P, blo * P, (bhi + 1) * P, masks))
            return jobs

        def prompt_jobs():
            return [(0, 0, 0, 256, []), (1, 128, 0, 256, []), (2, 256, 256, 512, []), (3, 384, 256, 512, [])]

        def attn_layer(l, lvl=9):
            odd = (l % 2 == 1)
            att_weights(l)
            load_ctx(l)
            nst = NS // TT
            load_x(0)
            rr = load_rope(0)
            norm_tile(0, l, 1, BY_)
            for i in range(nst):
                kv_tile(l, CTX + i * TT, 4 + i * 4, rr, None)
                if i + 1 < nst:
                    load_x(i + 1)
                    rr = load_rope((i + 1) * TT)
                    norm_tile(i + 1, l, 1, BY_)
            load_x(0)
            rr = load_rope(0)
            norm_tile(0, l, 1, BY_)
            q_tile_free(l, rr)
            for i in range(nst):
                jobs = window_jobs(i) if odd else full_jobs(NKT)
                nper = len(jobs)
                side = {}

                def at(c, frac, fn):
                    side.setdefault(c * nper + min(nper - 1, int(frac * nper)), []).append(fn)

                if i > 0:
                    for m in range(KC):
                        at(0, 0.05 + 0.05 * m, lambda m=m, i=i: oproj_m(i - 1, l, m))
                    at(0, 0.5, lambda i=i: store_x(i - 1))
                last_steps = None
                if i + 1 < nst:
                    box = {}

                    def ld(i=i, box=box):
                        load_x(i + 1)
                        box["r"] = load_rope((i + 1) * TT)
                    at(0, 0.55, ld)
                    rnext = (rcnt["r"]) % 2
                    nsteps = norm_steps(i + 1, l, 1, BY_, lean=True)
                    for n_, st in enumerate(nsteps):
                        at(0, 0.55 + 0.04 * n_, st)
                    for c in range(KC):
                        steps = q_chunk_steps(l, c, rnext)
                        if c < KC - 1:
                            for n_, st in enumerate(steps):
                                at(c + 1, 0.05 + 0.1 * n_, st)
                        else:
                            last_steps = steps
                attend(l, jobs, odd, side)
                if last_steps is not None:
                    run(last_steps)
            oproj_tile(nst - 1, l)
            store_x(nst - 1)
            for i in range(nst, NT):
                load_x(i)
            for i in range(nst, NT):
                norm_tile(i, l, 1, BY_)
                kv_tile(l, 0, 0, None, (i - nst) * 2)
                q_tile_free(l, None)
                attend(l, prompt_jobs(), odd, act_norm=True)
                oproj_tile(i, l)
                store_x(i)

        for l in range(depth):
            if "f1" in phases:
                ffn_pass(l, 0)
            for ph_ in phases:
                if ph_.startswith("att"):
                    attn_layer(l, int(ph_[3:]) if len(ph_) > 3 else 9)
            if "f2" in phases:
                ffn_pass(l, 1)

        A.top = base
        load_x(0)
        for i in range(NT):
            sl = xt[i % 2]
            st = stage[i % 2]
            if i + 1 < NT:
                load_x(i + 1)
            for s4 in range(4):
                for hf in range(2):
                    b = (s4 * 2 + hf) % 4
                    for kk in range(4):
                        k = hf * 4 + kk
                        S.add("pe", lambda e, sl=sl, k=k, kk=kk, s4=s4, b=b: e.transpose(
                            PSb[b][:, kk * P:(kk + 1) * P], sl.v[:, k, s4 * P:(s4 + 1) * P], ident_f),
                            reads=sl.ki(k) + identf.k(), writes=PK(b))
                    if hf == 0:
                        S.add("act", lambda e, st=st, s4=s4, b=b: e.copy(st.v[:, s4, 0:TT], PSb[b]),
                              reads=PK(b), writes=st.ki(s4))
                    else:
                        S.add("dve", lambda e, st=st, s4=s4, b=b: e.tensor_copy(st.v[:, s4, TT:D], PSb[b]),
                              reads=PK(b), writes=st.ki(s4))
            S.add("sp", lambda e, i=i, st=st: e.dma_start(
                out=y_d[i * TT:(i + 1) * TT, :].rearrange("(s p) d -> p s d", p=P), in_=st.v),
                reads=st.k(), writes=[("Y", i)], dma=True)

        S.number()
        block = ctx.enter_context(nc.Block())

        @block.tensor
        def _(e):
            S.emit("pe", e, sems, dsems)

        @block.scalar
        def _(e):
            S.emit("act", e, sems, dsems)

        @block.vector
        def _(e):
            S.emit("dve", e, sems, dsems)

        @block.gpsimd
        def _(e):
            S.emit("pool", e, sems, dsems)

        @block.sync
        def _(e):
            S.emit("sp", e, sems, dsems, final=True)

    return nc


def _consts():
    cst = np.zeros((P, 6, P), np.float32)
    idx = np.arange(P)
    cst[idx, 0, idx] = 1.0
    d = idx % HD
    e = d % 32
    partner = np.where(e < 16, idx + 16, idx - 16)
    cst[partner, 1, idx] = 1.0
    cst[:, 2, :] = (idx[:, None] // HD == idx[None, :] // HD) / float(HD)
    cst[:, 3, :] = 1.0 / D
    cst[:, 4, :] = np.where(idx[:, None] >= idx[None, :], 0.0, -30000.0)
    cst[:, 5, :] = np.where(idx[:, None] <= idx[None, :], 0.0, -30000.0)
    tok = np.arange(NS)
    row = (tok // 64).astype(np.float32)
    col = (tok % 64).astype(np.float32)
    freqs = (1.0 / np.power(np.float32(10000.0), np.arange(16, dtype=np.float32) / np.float32(16))).astype(np.float32)
    f = e % 16
    pos = np.where((d // 32 == 0)[:, None], row[None, :], col[None, :]).astype(np.float32)
    ang = (pos * freqs[f][:, None]).astype(np.float32)
    rc = np.cos(ang).astype(np.float32)
    sn = np.sin(ang).astype(np.float32)
    rs = np.where((e < 16)[:, None], -sn, sn).astype(np.float32)
    return cst, rc, rs


_NC_CACHE = {}


def _prep(x_prompt, x_sample, cache_k, cache_v, c, c_ctx, w_mod, b_mod, norm_g, w_qkv, w_o,
          q_norm_g, k_norm_g, sink, w_ffn_in, w_ffn_out):
    f32 = np.float32
    x_prompt = np.asarray(x_prompt, f32)
    x_sample = np.asarray(x_sample, f32)
    cache_k = np.asarray(cache_k, f32)
    cache_v = np.asarray(cache_v, f32)
    c = np.asarray(c, f32)
    c_ctx = np.asarray(c_ctx, f32)
    w_qkv = np.asarray(w_qkv, f32)
    w_o = np.asarray(w_o, f32)
    hq = np.concatenate([[cc, 8 + cc] for cc in range(8)])
    perm_q = (hq[:, None] * HD + np.arange(HD)[None, :]).reshape(-1)
    kvo = np.array([0, 2, 1, 3])
    perm_k = (kvo[:, None] * HD + np.arange(HD)[None, :]).reshape(-1)
    cols = np.concatenate([perm_q, D + perm_k, D + 256 + np.arange(256)])
    w_qkv_p = np.ascontiguousarray(w_qkv[:, :, cols])
    w_o_p = np.ascontiguousarray(w_o[:, perm_q, :])
    ck_p = np.ascontiguousarray(cache_k[:, :, :, kvo, :]).reshape(N_CORES, DEPTH, CTX, 256)
    cv_p = np.ascontiguousarray(cache_v).reshape(N_CORES, DEPTH, CTX, 256)
    cst, rc, rs = _consts()
    shared = {
        "w_mod": np.ascontiguousarray(np.asarray(w_mod, f32)),
        "b_mod": np.ascontiguousarray(np.asarray(b_mod, f32)),
        "norm_g": np.ascontiguousarray(np.asarray(norm_g, f32).reshape(DEPTH * 3, D)),
        "w_qkv": w_qkv_p,
        "w_o": w_o_p,
        "qk_g": np.ascontiguousarray(np.concatenate([np.asarray(q_norm_g, f32), np.asarray(k_norm_g, f32)], 0)),
        "sink": np.ascontiguousarray(np.asarray(sink, f32).reshape(1, 32)),
        "w_in": np.ascontiguousarray(np.asarray(w_ffn_in, f32)),
        "w_out": np.ascontiguousarray(np.asarray(w_ffn_out, f32)),
        "cst": cst, "rope_c": rc, "rope_s": rs,
    }
    in_maps = []
    for i in range(N_CORES):
        m = dict(shared)
        m["x"] = np.ascontiguousarray(np.concatenate(
            [x_sample[i], x_prompt[4 * i:4 * i + 4].reshape(NPR, D)], 0))
        m["ck"] = ck_p[i]
        m["cv"] = cv_p[i]
        m["cond"] = np.ascontiguousarray(np.stack([c[i], c_ctx], 0))
        in_maps.append(m)
    return in_maps


def kernel(x_prompt, x_sample, cache_k, cache_v, c, c_ctx, w_mod, b_mod, norm_g, w_qkv, w_o,
           q_norm_g, k_norm_g, sink, w_ffn_in, w_ffn_out):
    f32 = np.float32
    in_maps = _prep(x_prompt, x_sample, cache_k, cache_v, c, c_ctx, w_mod, b_mod, norm_g, w_qkv, w_o,
                    q_norm_g, k_norm_g, sink, w_ffn_in, w_ffn_out)
    if "nc" not in _NC_CACHE:
        _NC_CACHE["nc"] = build()
    nc = _NC_CACHE["nc"]
    res = run_bass_kernel_spmd(nc, in_maps, core_ids=list(range(N_CORES)))
    y_prompt = np.empty((32, 256, D), f32)
    y_sample = np.empty((8, NS, D), f32)
    new_k = np.empty((32, DEPTH, 256, NKV, HD), f32)
    new_v = np.empty((32, DEPTH, 256, NKV, HD), f32)
    for i in range(N_CORES):
        r = res.results[i]
        y = np.asarray(r["y"])
        y_sample[i] = y[:NS]
        y_prompt[4 * i:4 * i + 4] = y[NS:].reshape(4, 256, D)
        new_k[4 * i:4 * i + 4] = np.asarray(r["nk"]).reshape(4, DEPTH, 256, NKV, HD)
        new_v[4 * i:4 * i + 4] = np.asarray(r["nv"]).reshape(4, DEPTH, 256, NKV, HD)
    return (y_prompt, y_sample, new_k, new_v)
```

```python
import numpy as np
from contextlib import ExitStack
import concourse.bass as bass
import concourse.mybir as mybir
from concourse.bass_utils import run_bass_kernel_spmd

F32 = mybir.dt.float32
BF16 = mybir.dt.bfloat16
U8 = mybir.dt.uint8
AF = mybir.ActivationFunctionType
ALU = mybir.AluOpType

P = 128
D = 1024
KC = 8
DFF = 2816
JC = 22
NH = 16
NKV = 4
HD = 64
TT = 512
NS = 4096
NPR = 1024
NTOK = NS + NPR
NT = NTOK // TT
CTX = 512
DEPTH = 4
QKVC = 1536
NKT = (CTX + NS) // P
EPS = 1e-6
GRAN = 256
NDQ = 8
N_CORES = 8

ENGS = ("pe", "act", "dve", "pool", "sp")


class Op:
    __slots__ = ("eng", "fn", "deps", "signal", "idx", "dma", "dseq")

    def __init__(self, eng, fn, dma):
        self.eng = eng
        self.fn = fn
        self.dma = dma
        self.deps = ()
        self.signal = False
        self.idx = 0
        self.dseq = -1


class Sched:
    def __init__(self):
        self.streams = {e: [] for e in ENGS}
        self.lastw = {}
        self.readers = {}
        self.dma_count = {e: 0 for e in ENGS}

    def add(self, eng, fn, reads=(), writes=(), dma=False):
        o = Op(eng, fn, dma)
        pr = [k for k in reads if k[0] == "P"]
        if pr:
            reads = [k for k in reads if k[0] != "P"]
            writes = list(writes) + pr
        raw = set()
        war = set()
        for k in reads:
            w = self.lastw.get(k)
            if w is not None:
                raw.add(w)
        for k in writes:
            w = self.lastw.get(k)
            if w is not None:
                raw.add(w)
            rd = self.readers.get(k)
            if rd:
                war.update(rd.values())
        real = []
        for d in raw:
            if d.dma or dma:
                real.append(d)
            elif d.eng == eng:
                if eng != "pe":
                    real.append(d)
            else:
                real.append(d)
        for d in war:
            if d in raw:
                continue
            if d.dma or dma:
                real.append(d)
            elif d.eng != eng:
                real.append(d)
        for d in real:
            d.signal = True
        o.deps = real
        for k in reads:
            rd = self.readers.get(k)
            if rd is None:
                rd = self.readers[k] = {}
            rd[("d", id(o)) if dma else eng] = o
        for k in writes:
            self.lastw[k] = o
            self.readers[k] = {}
        if dma:
            o.dseq = self.dma_count[eng]
            self.dma_count[eng] += 1
            o.signal = True
        self.streams[eng].append(o)
        return o

    def number(self):
        for e in ENGS:
            c = 0
            for o in self.streams[e]:
                if (not o.dma) and o.signal:
                    c += 1
                    o.idx = c

    def emit(self, eng, E, sems, dsems, final=False):
        waited = {}
        for o in self.streams[eng]:
            need = {}
            for d in o.deps:
                if d.dma:
                    s = dsems[d.eng][d.dseq % NDQ]
                    v = 16 * (d.dseq // NDQ + 1)
                else:
                    s = sems[d.eng]
                    v = d.idx
                cur = need.get(s.name)
                if cur is None or cur[1] < v:
                    need[s.name] = (s, v)
            if o.dma and o.dseq >= NDQ:
                s = dsems[eng][o.dseq % NDQ]
                v = 16 * (o.dseq // NDQ)
                cur = need.get(s.name)
                if cur is None or cur[1] < v:
                    need[s.name] = (s, v)
            for name, (s, v) in need.items():
                if waited.get(name, 0) < v:
                    E.wait_ge(s, v)
                    waited[name] = v
            ins = o.fn(E)
            if o.dma:
                ins.then_inc(dsems[eng][o.dseq % NDQ], 16)
            elif o.signal:
                ins.then_inc(sems[eng], 1)
        if final:
            for q in ENGS:
                n = self.dma_count[q]
                for i in range(min(NDQ, n)):
                    cnt = (n - i + NDQ - 1) // NDQ
                    if cnt > 0:
                        E.wait_ge(dsems[q][i], 16 * cnt)


class Buf:
    arena = None

    def __init__(self, off, shape, dt):
        self.off = off
        self.shape = tuple(shape)
        self.dt = dt
        self.esz = 4 if dt == F32 else 2
        n = 1
        for s in shape:
            n *= s
        self.n = n
        self.nbytes = n * self.esz
        self._ap = None

    @property
    def v(self):
        if self._ap is None:
            a = Buf.arena[:, self.off:self.off + self.nbytes].bitcast(self.dt)
            if len(self.shape) == 2:
                a = a.rearrange("p (a b) -> p a b", a=self.shape[0])
            elif len(self.shape) == 3:
                a = a.rearrange("p (a b c) -> p a b c", a=self.shape[0], b=self.shape[1])
            elif len(self.shape) == 4:
                a = a.rearrange("p (a b c d) -> p a b c d", a=self.shape[0], b=self.shape[1], c=self.shape[2])
            self._ap = a
        return self._ap

    def k(self, lo=0, hi=None):
        if hi is None:
            hi = self.n
        b0 = (self.off + lo * self.esz) // GRAN
        b1 = (self.off + hi * self.esz - 1) // GRAN
        return [("A", g) for g in range(b0, b1 + 1)]

    def ki(self, *idx):
        stride = self.n
        lo = 0
        for d, i in enumerate(idx):
            stride //= self.shape[d]
            lo += i * stride
        return self.k(lo, lo + stride)


class Alloc:
    def __init__(self):
        self.top = 0
        self.peak = 0

    def __call__(self, shape, dt, align=1024):
        off = (self.top + align - 1) // align * align
        b = Buf(off, shape, dt)
        self.top = off + b.nbytes
        self.peak = max(self.peak, self.top)
        return b


def PK(*banks):
    return [("P", b) for b in banks]


def build(depth=DEPTH, dbg=False, phases=("f1", "att", "f2"), mods=True):
    nc = bass.Bass("TRN2", target_bir_lowering=False)
    S = Sched()

    def din(name, shape):
        return nc.dram_tensor(name, list(shape), F32, kind="ExternalInput").ap()

    def dout(name, shape):
        return nc.dram_tensor(name, list(shape), F32, kind="ExternalOutput").ap()

    x_d = din("x", [NTOK, D])
    ck_d = din("ck", [DEPTH, CTX, 256])
    cv_d = din("cv", [DEPTH, CTX, 256])
    cond_d = din("cond", [2, D])
    wmod_d = din("w_mod", [DEPTH, D, 9 * D])
    bmod_d = din("b_mod", [DEPTH, 9 * D])
    ng_d = din("norm_g", [DEPTH * 3, D])
    wqkv_d = din("w_qkv", [DEPTH, D, QKVC])
    wo_d = din("w_o", [DEPTH, D, D])
    qkg_d = din("qk_g", [8, HD])
    sink_d = din("sink", [1, 32])
    win_d = din("w_in", [DEPTH, 2, D, 2 * DFF])
    wout_d = din("w_out", [DEPTH, 2, DFF, D])
    cst_d = din("cst", [P, 6, P])
    rc_d = din("rope_c", [P, NS])
    rs_d = din("rope_s", [P, NS])
    y_d = dout("y", [NTOK, D])
    nk_d = dout("nk", [4, DEPTH, 256, 256])
    nv_d = dout("nv", [4, DEPTH, 256, 256])
    xT_d = nc.dram_tensor("xT_scr", [D, NTOK], F32).ap()
    xT_v = xT_d.rearrange("(k p) t -> p k t", p=P)

    A = Alloc()
    xt = [A([KC, TT], F32) for _ in range(2)]
    xn = A([KC, TT], BF16)
    MODV = A([DEPTH * 2 * 3 * 3, KC], F32)
    identf = A([P], F32)
    cb = A([6, P], BF16)
    epsT = A([1], F32, align=64)
    ones2 = A([2], BF16, align=64)
    es = A([32], F32, align=64)
    qkg = A([8], F32, align=64)
    sd = A([TT], F32)
    rstd = sd
    tring = [A([TT], F32) for _ in range(2)]
    sqring = [A([TT], BF16) for _ in range(3)]
    base = A.top
    Win = A([KC, 2 * DFF], BF16)
    Wout = A([JC, D], BF16)
    hT = A([JC // 2, TT], BF16)
    sg = [A([TT], F32) for _ in range(2)]
    ffn_top = A.top
    A.top = base
    KT = A([2, CTX + NS], BF16)
    VA = A([NKT, NKV, P], BF16)
    Wqkv = A([KC, QKVC], BF16)
    Wo = A([KC, D], BF16)
    QT = A([KC, TT], BF16)
    AT = A([KC, TT], BF16)
    PT = [A([2, TT], BF16) for _ in range(3)]
    osb = [A([TT], F32) for _ in range(2)]
    ropeC = [A([TT], F32) for _ in range(2)]
    ropeS = [A([TT], F32) for _ in range(2)]
    zr = [A([TT], F32) for _ in range(2)]
    zb = [A([TT], BF16) for _ in range(2)]
    ra_off = A.top
    ra = [A([TT], F32) for _ in range(2)]
    rb_off = A.top
    rb = [A([TT], F32) for _ in range(2)]
    sdq = [A([TT], F32) for _ in range(2)]
    rq = sdq
    rec = [A([TT], F32) for _ in range(2)]
    rs_ = rec
    ckst = Buf(osb[0].off, [4, 256], BF16)
    vst = Buf(rb_off, [4, 256], F32)
    nkst = Buf(ra_off, [4, 256], F32)
    att_top = A.top
    A.top = base
    stage = [A([4, D], F32) for _ in range(2)]
    wm = [A([KC, D], BF16) for _ in range(2)]
    bm = [A([D], BF16) for _ in range(2)]
    ctok = A([D], F32)
    ngtok = A([D], F32)
    qktok = A([P], F32)
    scb = A([KC, 2], BF16, align=64)
    ngT = A([KC, 12], F32, align=64)
    modraw = A([72, 2], F32, align=64)
    estmp = A([32], F32, align=64)
    qktmp = A([8], F32, align=64)
    setup_top = A.top
    total = max(ffn_top, att_top, setup_top)
    total = (total + 1023) // 1024 * 1024
    assert total <= 212000, total

    with ExitStack() as ctx:
        arena_t = ctx.enter_context(nc.sbuf_tensor("arena", [P, total], U8))
        ps_t = ctx.enter_context(nc.psum_tensor("ps", [P, 8 * TT], F32))
        sems = {e: ctx.enter_context(nc.semaphore("s_" + e)) for e in ENGS}
        dsems = {e: [ctx.enter_context(nc.semaphore("d_%s%d" % (e, i))) for i in range(NDQ)]
                 for e in ("sp", "pool")}
        for e in ENGS:
            dsems.setdefault(e, dsems["sp"])
        Buf.arena = arena_t[:, :]
        PSb = [ps_t[:, b * TT:(b + 1) * TT] for b in range(8)]

        def PSv(b0, nb):
            return ps_t[:, b0 * TT:(b0 + nb) * TT]

        def mod(l, j, s, w):
            r = ((l * 2 + j) * 3 + s) * 3 + w
            return MODV.v[:, r, :]

        ident_f = identf.v
        ident_b = cb.v[:, 0, :]
        perm_b = cb.v[:, 1, :]
        blk_b = cb.v[:, 2, :]
        ones_b = cb.v[:, 3, :]
        mL = cb.v[:, 4, :]
        mR = cb.v[:, 5, :]

        S.add("sp", lambda e: e.dma_start(out=identf.v, in_=cst_d[:, 0, :]), writes=identf.k(), dma=True)
        S.add("pool", lambda e: e.dma_start(out=cb.v, in_=cst_d), writes=cb.k(), dma=True)
        S.add("pool", lambda e: e.memset(epsT.v, EPS), writes=epsT.k())
        S.add("pool", lambda e: e.memset(ones2.v, 1.0), writes=ones2.k())
        S.add("sp", lambda e: e.dma_start(out=ctok.v[0:2, :], in_=cond_d), writes=ctok.k(), dma=True)
        S.add("sp", lambda e: e.dma_start(out=ngtok.v[0:12, :], in_=ng_d), writes=ngtok.k(), dma=True)
        for r0, c0 in ((0, 0), (0, 64)):
            S.add("sp", lambda e, r0=r0, c0=c0: e.dma_start(out=qktok.v[0:8, c0:c0 + 64], in_=qkg_d),
                  writes=qktok.k(), dma=True)
        S.add("sp", lambda e: e.dma_start(out=estmp.v, in_=sink_d.partition_broadcast(P)),
              writes=estmp.k(), dma=True)
        S.add("act", lambda e: e.activation(out=es.v, in_=estmp.v, func=AF.Exp), reads=estmp.k(), writes=es.k())
        for k in range(KC):
            S.add("pe", lambda e, k=k: e.transpose(PSb[1][:, k * 2:(k + 1) * 2], ctok.v[0:2, k * P:(k + 1) * P],
                                                    ident_f[0:2, 0:2]),
                  reads=ctok.k() + identf.k(), writes=PK(1))
        S.add("act", lambda e: e.activation(out=scb.v, in_=PSb[1][:, 0:16].rearrange("p (k j) -> p k j", j=2),
                                            func=AF.Silu), reads=PK(1), writes=scb.k())
        for k in range(KC):
            S.add("pe", lambda e, k=k: e.transpose(PSb[2][:, k * 12:(k + 1) * 12], ngtok.v[0:12, k * P:(k + 1) * P],
                                                    ident_f[0:12, 0:12]),
                  reads=ngtok.k() + identf.k(), writes=PK(2))
        S.add("dve", lambda e: e.tensor_copy(ngT.v, PSb[2][:, 0:96].rearrange("p (k j) -> p k j", j=12)),
              reads=PK(2), writes=ngT.k())
        S.add("pe", lambda e: e.transpose(PSb[3][:, 0:8], qktok.v[0:8, :], ident_f[0:8, 0:8]),
              reads=qktok.k() + identf.k(), writes=PK(3))
        S.add("dve", lambda e: e.tensor_copy(qktmp.v, PSb[3][:, 0:8]), reads=PK(3), writes=qktmp.k())
        S.add("dve", lambda e: e.tensor_scalar(out=qkg.v[:, 0:4], in0=qktmp.v[:, 0:4], scalar1=HD ** -0.5,
                                               scalar2=None, op0=ALU.mult), reads=qktmp.k(), writes=qkg.k())
        S.add("dve", lambda e: e.tensor_copy(qkg.v[:, 4:8], qktmp.v[:, 4:8]), reads=qktmp.k(), writes=qkg.k())

        def t0_tile(i):
            st = stage[i % 2]
            sl = xt[i % 2]
            S.add("sp", lambda e, i=i, st=st: e.dma_start(
                out=st.v, in_=x_d[i * TT:(i + 1) * TT, :].rearrange("(s p) d -> p s d", p=P)),
                writes=st.k(), dma=True)
            for k in range(KC):
                b = k % 4
                for s in range(4):
                    S.add("pe", lambda e, st=st, k=k, s=s, b=b: e.transpose(
                        PSb[b][:, s * P:(s + 1) * P], st.v[:, s, k * P:(k + 1) * P], ident_f),
                        reads=st.ki(s) + identf.k(), writes=PK(b))
                if k % 2 == 0:
                    S.add("act", lambda e, sl=sl, k=k, b=b: e.copy(sl.v[:, k, :], PSb[b]),
                          reads=PK(b), writes=sl.ki(k))
                else:
                    S.add("dve", lambda e, sl=sl, k=k, b=b: e.tensor_copy(sl.v[:, k, :], PSb[b]),
                          reads=PK(b), writes=sl.ki(k))
            S.add("sp", lambda e, i=i, sl=sl: e.dma_start(out=xT_v[:, :, i * TT:(i + 1) * TT], in_=sl.v),
                  reads=sl.k(), writes=[("X", i)], dma=True)


        def mod_block(l, v):
            mb = 4 + l % 2
            w_ = wm[v % 2]
            b_ = bm[v % 2]
            S.add("pool", lambda e, l=l, v=v, w_=w_: e.dma_start(
                out=w_.v, in_=wmod_d[l, :, v * D:(v + 1) * D].rearrange("(k p) c -> p k c", p=P)),
                writes=w_.k(), dma=True)
            S.add("pool", lambda e, l=l, v=v, b_=b_: e.dma_start(
                out=b_.v[0:1, :], in_=bmod_d[l:l + 1, v * D:(v + 1) * D]),
                writes=b_.k(), dma=True)
            for m in range(KC):
                c = v * KC + m
                for k in range(KC):
                    S.add("pe", lambda e, w_=w_, m=m, k=k, c=c, mb=mb: e.matmul(
                        PSb[mb][:, c * 2:c * 2 + 2], w_.v[:, k, m * P:(m + 1) * P], scb.v[:, k, :],
                        start=(c == 0 and k == 0), stop=False, skip_group_check=True),
                        reads=w_.ki(k) + scb.k(), writes=PK(mb))
                S.add("pe", lambda e, b_=b_, m=m, c=c, mb=mb: e.matmul(
                    PSb[mb][:, c * 2:c * 2 + 2], b_.v[0:1, m * P:(m + 1) * P], ones2.v[0:1, :],
                    start=False, stop=True, skip_group_check=True),
                    reads=b_.k() + ones2.k(), writes=PK(mb))

        def mod_final(l):
            mb = 4 + l % 2
            S.add("dve", lambda e, mb=mb: e.tensor_copy(modraw.v, PSb[mb][:, 0:144].rearrange("p (c j) -> p c j", j=2)),
                  reads=PK(mb), writes=modraw.k())
            for j in range(2):
                for s in range(3):
                    S.add("dve", lambda e, l=l, j=j, s=s: e.scalar_tensor_tensor(
                        out=mod(l, j, s, 0), in0=modraw.v[:, (3 * s + 1) * 8:(3 * s + 2) * 8, j], scalar=1.0,
                        in1=ngT.v[:, :, l * 3 + s], op0=ALU.add, op1=ALU.mult),
                        reads=modraw.k() + ngT.k(), writes=MODV.k())
                    S.add("dve", lambda e, l=l, j=j, s=s: e.tensor_copy(
                        mod(l, j, s, 1), modraw.v[:, (3 * s) * 8:(3 * s + 1) * 8, j]),
                        reads=modraw.k(), writes=MODV.k())
                    S.add("dve", lambda e, l=l, j=j, s=s: e.tensor_scalar(
                        out=mod(l, j, s, 2), in0=modraw.v[:, (3 * s + 2) * 8:(3 * s + 3) * 8, j],
                        scalar1=(1.0 if s == 1 else 0.5), scalar2=None, op0=ALU.mult),
                        reads=modraw.k(), writes=MODV.k())


        mod_work = []
        for l in range(depth if mods else 0):
            for v in range(9):
                mod_work.append(lambda l=l, v=v: mod_block(l, v))
            mod_work.append(lambda l=l: mod_final(l))
        per = (len(mod_work) + NT - 1) // NT
        for i in range(NT):
            t0_tile(i)
            for fn in mod_work[i * per:(i + 1) * per]:
                fn()

        def load_x(i):
            sl = xt[i % 2]
            S.add("sp", lambda e, i=i, sl=sl: e.dma_start(out=sl.v, in_=xT_v[:, :, i * TT:(i + 1) * TT]),
                  reads=[("X", i)], writes=sl.k(), dma=True)

        def store_x(i):
            sl = xt[i % 2]
            S.add("sp", lambda e, i=i, sl=sl: e.dma_start(out=xT_v[:, :, i * TT:(i + 1) * TT], in_=sl.v),
                  reads=sl.k(), writes=[("X", i)], dma=True)

        cnt = {"sq": 0, "t": 0}

        def norm_steps(i, l, s, msb=6, lean=False):
            j = 0 if i < NS // TT else 1
            sl = xt[i % 2]

            def st_sq(k):
                q = sqring[cnt["sq"] % 3]
                cnt["sq"] += 1
                if lean:
                    S.add("dve", lambda e: e.tensor_tensor(out=q.v, in0=sl.v[:, k, :], in1=sl.v[:, k, :], op=ALU.mult),
                          reads=sl.ki(k), writes=q.k())
                else:
                    S.add("act", lambda e: e.activation(out=q.v, in_=sl.v[:, k, :], func=AF.Square),
                          reads=sl.ki(k), writes=q.k())
                S.add("pe", lambda e: e.matmul(PSb[msb], ones_b, q.v, start=(k == 0), stop=(k == KC - 1)),
                      reads=q.k() + cb.k(), writes=PK(msb))

            def st_rs():
                S.add("act", lambda e: e.activation(out=sd.v, in_=PSb[msb], func=AF.Ln, bias=epsT.v, scale=1.0),
                      reads=PK(msb) + epsT.k(), writes=sd.k())
                S.add("act", lambda e: e.activation(out=sd.v, in_=sd.v, func=AF.Exp, scale=-0.5),
                      reads=sd.k(), writes=sd.k())

            def st_xn():
                for k in range(KC):
                    t = tring[cnt["t"] % 2]
                    cnt["t"] += 1
                    S.add("dve", lambda e, k=k, t=t: e.tensor_tensor(out=t.v, in0=sl.v[:, k, :], in1=rstd.v, op=ALU.mult),
                          reads=sl.ki(k) + rstd.k(), writes=t.k())
                    if k % 2 == 0 or lean:
                        S.add("pool", lambda e, k=k, t=t: e.tensor_scalar(
                            out=xn.v[:, k, :], in0=t.v, scalar1=mod(l, j, s, 0)[:, k:k + 1], scalar2=mod(l, j, s, 1)[:, k:k + 1],
                            op0=ALU.mult, op1=ALU.add), reads=t.k() + MODV.k(), writes=xn.ki(k))
                    else:
                        S.add("act", lambda e, k=k, t=t: e.activation(
                            out=xn.v[:, k, :], in_=t.v, func=AF.Identity, scale=mod(l, j, s, 0)[:, k:k + 1],
                            bias=mod(l, j, s, 1)[:, k:k + 1]), reads=t.k() + MODV.k(), writes=xn.ki(k))

            return [(lambda k=k: st_sq(k)) for k in range(KC)] + [st_rs, st_xn]

        def norm_tile(i, l, s, msb=6):
            for st in norm_steps(i, l, s, msb):
                st()

        def residual(i, l, s, m, bank):
            j = 0 if i < NS // TT else 1
            sl = xt[i % 2]
            S.add("dve", lambda e, sl=sl, m=m, bank=bank, l=l, j=j, s=s: e.scalar_tensor_tensor(
                out=sl.v[:, m, :], in0=PSb[bank], scalar=mod(l, j, s, 2)[:, m:m + 1], in1=sl.v[:, m, :],
                op0=ALU.mult, op1=ALU.add), reads=PK(bank) + sl.ki(m) + MODV.k(), writes=sl.ki(m))

        def ffn_pass(l, f):
            s = 0 if f == 0 else 2
            HC = DFF // 2
            for hf in range(2):
                for gu_ in range(2):
                    for k in range(KC):
                        c0 = gu_ * DFF + hf * HC
                        S.add("pool", lambda e, k=k, c0=c0: e.dma_start(
                            out=Win.v[:, k, c0:c0 + HC], in_=win_d[l, f, k * P:(k + 1) * P, c0:c0 + HC]),
                            writes=Win.k(k * 2 * DFF + c0, k * 2 * DFF + c0 + HC), dma=True)
                j0 = hf * 11
                S.add("pool", lambda e, j0=j0: e.dma_start(
                    out=Wout.v[:, j0:j0 + 11, :],
                    in_=wout_d[l, f, j0 * P:(j0 + 11) * P, :].rearrange("(j p) n -> p j n", p=P)),
                    writes=Wout.k(j0 * D, (j0 + 11) * D), dma=True)

            HJ = JC // 2

            def gu(i, hf, inter=()):
                for jj in range(HJ):
                    if jj < len(inter):
                        inter[jj]()
                    jc = hf * HJ + jj
                    gb = (jc % 2) * 2
                    ub = gb + 1
                    for (bank, c0) in ((gb, jc * P), (ub, DFF + jc * P)):
                        for k in range(KC):
                            S.add("pe", lambda e, bank=bank, c0=c0, k=k: e.matmul(
                                PSb[bank], Win.v[:, k, c0:c0 + P], xn.v[:, k, :], start=(k == 0), stop=(k == KC - 1)),
                                reads=Win.k(k * 2 * DFF + c0, k * 2 * DFF + c0 + P) + xn.ki(k), writes=PK(bank))
                    g_ = sg[jc % 2]
                    S.add("act", lambda e, g_=g_, gb=gb: e.activation(out=g_.v, in_=PSb[gb], func=AF.Silu),
                          reads=PK(gb), writes=g_.k())
                    S.add("dve", lambda e, g_=g_, ub=ub, jj=jj: e.tensor_tensor(
                        out=hT.v[:, jj, :], in0=PSb[ub], in1=g_.v, op=ALU.mult),
                        reads=PK(ub) + g_.k(), writes=hT.ki(jj))

            def wout(i, hf):
                for m in range(KC):
                    bank = 4 + m % 2
                    for jj in range(HJ):
                        jc = hf * HJ + jj
                        S.add("pe", lambda e, bank=bank, jc=jc, jj=jj, m=m: e.matmul(
                            PSb[bank], Wout.v[:, jc, m * P:(m + 1) * P], hT.v[:, jj, :],
                            start=(jj == 0), stop=(jj == HJ - 1)),
                            reads=Wout.k(jc * D + m * P, jc * D + (m + 1) * P) + hT.ki(jj), writes=PK(bank))
                    residual(i, l, s, m, bank)

            load_x(0)
            norm_tile(0, l, s)
            for i in range(NT):
                nxt = []
                if i + 1 < NT:
                    load_x(i + 1)
                    nxt = norm_steps(i + 1, l, s)
                gu(i, 0)
                wout(i, 0)
                gu(i, 1, nxt[:KC])
                for st in nxt[KC:]:
                    st()
                wout(i, 1)
                store_x(i)

        def att_weights(l):
            S.add("pool", lambda e: e.dma_start(out=Wqkv.v, in_=wqkv_d[l].rearrange("(k p) c -> p k c", p=P)),
                  writes=Wqkv.k(), dma=True)
            S.add("pool", lambda e: e.dma_start(out=Wo.v, in_=wo_d[l].rearrange("(c p) n -> p c n", p=P)),
                  writes=Wo.k(), dma=True)
            S.add("pool", lambda e: e.memset(VA.v[:, :, :, HD:P], 1.0), writes=VA.k())

        def load_ctx(l):
            for t in range(4):
                S.add("pool", lambda e, t=t: e.dma_start(
                    out=VA.v[:, t, :, 0:HD],
                    in_=cv_d[l, t * P:(t + 1) * P, :].rearrange("p (g d) -> p g d", g=NKV)),
                    writes=VA.ki(t), dma=True)
            S.add("pool", lambda e: e.dma_start(out=ckst.v, in_=ck_d[l].rearrange("(t p) c -> p t c", p=P)),
                  writes=ckst.k(), dma=True)
            for t in range(4):
                for i2 in range(2):
                    S.add("pe", lambda e, t=t, i2=i2: e.transpose(
                        PSb[i2].bitcast(BF16)[:, t * P:(t + 1) * P], ckst.v[:, t, i2 * P:(i2 + 1) * P], ident_b),
                        reads=ckst.ki(t) + cb.k(), writes=PK(i2))
            for i2 in range(2):
                S.add("dve", lambda e, i2=i2: e.tensor_copy(KT.v[:, i2, 0:CTX], PSb[i2].bitcast(BF16)[:, 0:CTX]),
                      reads=PK(i2), writes=KT.k(i2 * (CTX + NS), i2 * (CTX + NS) + CTX))

        rcnt = {"c": 0, "r": 0}
        BX_, BY_ = 6, 7

        def load_rope(tok0):
            r = rcnt["r"] % 2
            rcnt["r"] += 1
            S.add("sp", lambda e: e.dma_start(out=ropeC[r].v, in_=rc_d[:, tok0:tok0 + TT]),
                  writes=ropeC[r].k(), dma=True)
            S.add("sp", lambda e: e.dma_start(out=ropeS[r].v, in_=rs_d[:, tok0:tok0 + TT]),
                  writes=ropeS[r].k(), dma=True)
            return r

        def qk_steps(l, wcol, gcol, rope, dst_ap, dst_keys, banks=(6, 7)):
            c = rcnt["c"]
            rcnt["c"] += 1
            r2 = c % 2
            z = zr[r2]
            q = sqring[c % 3]
            BX, BY = banks
            rr = rope

            def s_proj():
                for k in range(KC):
                    S.add("pe", lambda e, k=k: e.matmul(PSb[BX], Wqkv.v[:, k, wcol:wcol + P], xn.v[:, k, :],
                                                        start=(k == 0), stop=(k == KC - 1)),
                          reads=Wqkv.k(k * QKVC + wcol, k * QKVC + wcol + P) + xn.ki(k), writes=PK(BX))

            def s_sq():
                S.add("act", lambda e: e.activation(out=q.v, in_=PSb[BX], func=AF.Square),
                      reads=PK(BX), writes=q.k())

            def s_ms():
                S.add("pe", lambda e: e.matmul(PSb[BY], blk_b, q.v, start=True, stop=True),
                      reads=q.k() + cb.k(), writes=PK(BY))

            def s_rs():
                S.add("act", lambda e: e.activation(out=sdq[r2].v, in_=PSb[BY], func=AF.Ln, bias=epsT.v, scale=1.0),
                      reads=PK(BY) + epsT.k(), writes=sdq[r2].k())
                S.add("act", lambda e: e.activation(out=sdq[r2].v, in_=sdq[r2].v, func=AF.Exp, scale=-0.5),
                      reads=sdq[r2].k(), writes=sdq[r2].k())

            def s_z():
                S.add("dve", lambda e: e.scalar_tensor_tensor(
                    out=z.v, in0=PSb[BX], scalar=qkg.v[:, gcol:gcol + 1], in1=sdq[r2].v, op0=ALU.mult, op1=ALU.mult),
                    reads=PK(BX) + sdq[r2].k() + qkg.k(), writes=z.k())

            if rr is None:
                def s_out():
                    S.add("act", lambda e: e.copy(dst_ap, z.v), reads=z.k(), writes=dst_keys)
                return [s_proj, s_sq, s_ms, s_rs, s_z, s_out], z

            def s_zb():
                S.add("dve", lambda e: e.tensor_copy(zb[r2].v, z.v), reads=z.k(), writes=zb[r2].k())

            def s_perm():
                S.add("pe", lambda e: e.matmul(PSb[BY], perm_b, zb[r2].v, start=True, stop=True),
                      reads=zb[r2].k() + cb.k(), writes=PK(BY))

            def s_rope():
                S.add("pool", lambda e: e.tensor_tensor(out=ra[r2].v, in0=z.v, in1=ropeC[rr].v, op=ALU.mult),
                      reads=z.k() + ropeC[rr].k(), writes=ra[r2].k())
                S.add("dve", lambda e: e.tensor_tensor(out=rb[r2].v, in0=PSb[BY], in1=ropeS[rr].v, op=ALU.mult),
                      reads=PK(BY) + ropeS[rr].k(), writes=rb[r2].k())
                S.add("dve", lambda e: e.tensor_tensor(out=dst_ap, in0=ra[r2].v, in1=rb[r2].v, op=ALU.add),
                      reads=ra[r2].k() + rb[r2].k(), writes=dst_keys)

            return [s_proj, s_sq, s_ms, s_rs, s_z, s_zb, s_perm, s_rope], z

        def run(steps):
            for st in steps:
                st()

        def run_interleaved(lists, skew=2):
            T = max(len(x) + c * skew for c, x in enumerate(lists))
            for t in range(T):
                for c, x in enumerate(lists):
                    idx = t - c * skew
                    if 0 <= idx < len(x):
                        x[idx]()

        BANKPAIRS = ((6, 7), (0, 1), (2, 3))

        def kv_tile(l, key0, kt0, rope, out_seq0):
            lists = []
            for i2 in range(2):
                lo = i2 * (CTX + NS) + key0
                steps, z = qk_steps(l, D + i2 * P, 4 + l, rope, KT.v[:, i2, key0:key0 + TT], KT.k(lo, lo + TT),
                                    banks=BANKPAIRS[i2])
                if out_seq0 is not None:
                    def s_nk(i2=i2, z=z):
                        tb = 2 + i2
                        for s4 in range(4):
                            S.add("pe", lambda e, s4=s4: e.transpose(
                                PSb[tb][:, s4 * P:(s4 + 1) * P], z.v[:, s4 * P:(s4 + 1) * P], ident_f),
                                reads=z.k() + identf.k(), writes=PK(tb))
                        for gg in range(2):
                            g = i2 + 2 * gg
                            S.add("dve", lambda e, g=g, gg=gg: e.tensor_copy(
                                nkst.v[:, :, g * HD:(g + 1) * HD],
                                PSb[tb].rearrange("p (s c) -> p s c", s=4)[:, :, gg * HD:(gg + 1) * HD]),
                                reads=PK(tb), writes=nkst.k())
                    steps = steps + [s_nk]
                lists.append(steps)

            def s_v():
                for s4 in range(4):
                    vb = 4 + s4 // 2
                    c0 = (s4 % 2) * 256
                    for k in range(KC):
                        S.add("pe", lambda e, s4=s4, k=k, vb=vb, c0=c0: e.matmul(
                            PSb[vb][:, c0:c0 + 256], xn.v[:, k, s4 * P:(s4 + 1) * P], Wqkv.v[:, k, D + 256:QKVC],
                            start=(k == 0), stop=(k == KC - 1), skip_group_check=True),
                            reads=xn.ki(k) + Wqkv.k(k * QKVC + D + 256, (k + 1) * QKVC), writes=PK(vb))
                    S.add("act", lambda e, s4=s4, vb=vb, c0=c0: e.copy(
                        VA.v[:, kt0 + s4, :, 0:HD], PSb[vb][:, c0:c0 + 256].rearrange("p (g d) -> p g d", g=NKV)),
                        reads=PK(vb), writes=VA.ki(kt0 + s4))
                    if out_seq0 is not None:
                        S.add("dve", lambda e, s4=s4, vb=vb, c0=c0: e.tensor_copy(vst.v[:, s4, :], PSb[vb][:, c0:c0 + 256]),
                              reads=PK(vb), writes=vst.ki(s4))
            lists.append([s_v])
            run_interleaved(lists, skew=1)
            if out_seq0 is not None:
                for q2 in range(2):
                    S.add("sp", lambda e, q2=q2: e.dma_start(
                        out=nk_d[out_seq0 + q2, l].rearrange("(h p) c -> p h c", p=P), in_=nkst.v[:, 2 * q2:2 * q2 + 2, :]),
                        reads=nkst.k(), writes=[("NK", out_seq0 + q2, l)], dma=True)
                    S.add("sp", lambda e, q2=q2: e.dma_start(
                        out=nv_d[out_seq0 + q2, l].rearrange("(h p) c -> p h c", p=P), in_=vst.v[:, 2 * q2:2 * q2 + 2, :]),
                        reads=vst.k(), writes=[("NV", out_seq0 + q2, l)], dma=True)

        def q_chunk_steps(l, c, rope, banks=(6, 7)):
            return qk_steps(l, c * P, l, rope, QT.v[:, c, :], QT.ki(c), banks=banks)[0]

        def q_tile_free(l, rope):
            run_interleaved([q_chunk_steps(l, c, rope, BANKPAIRS[c % 3]) for c in range(KC)], skew=2)

        acnt = {"n": 0}

        def attend(l, jobs, use_sink, side=None, act_norm=False):
            items = [(c, jb, n == 0, n == len(jobs) - 1) for c in range(KC) for n, jb in enumerate(jobs)]
            slot = []
            pslot = []
            for _ in items:
                slot.append(acnt["n"] % 2)
                pslot.append(acnt["n"] % 3)
                acnt["n"] += 1

            def s_mm(a):
                c, (kt, key0, qlo, qhi, masks), first, last = items[a]
                g = slot[a]
                i2 = c // 4
                kk0 = i2 * (CTX + NS) + key0
                for h2 in range(2):
                    pr = slice(h2 * HD, (h2 + 1) * HD)
                    bank = 2 * g + h2
                    S.add("pe", lambda e, pr=pr, bank=bank: e.matmul(
                        PSb[bank][:, qlo:qhi], KT.v[pr, i2, key0:key0 + P], QT.v[pr, c, qlo:qhi],
                        start=True, stop=(not masks), skip_group_check=True),
                        reads=KT.k(kk0, kk0 + P) + QT.ki(c), writes=PK(bank))
                    for mi, (blk, side_) in enumerate(masks):
                        mk = mL if side_ == "L" else mR
                        S.add("pe", lambda e, bank=bank, blk=blk, mk=mk, mi=mi: e.matmul(
                            PSb[bank][:, blk * P:(blk + 1) * P], ident_b, mk,
                            start=False, stop=(mi == len(masks) - 1), skip_group_check=True),
                            reads=cb.k(), writes=PK(bank))

            pending = []

            def do_exp(a):
                c, (kt, key0, qlo, qhi, masks), first, last = items[a]
                g = slot[a]
                pt = PT[pslot[a]]
                S.add("act", lambda e: e.activation(
                    out=pt.v[:, :, qlo:qhi], in_=PSv(2 * g, 2).rearrange("p (h q) -> p h q", h=2)[:, :, qlo:qhi],
                    func=AF.Exp), reads=PK(2 * g, 2 * g + 1), writes=pt.k())

            def do_pv(a):
                c, (kt, key0, qlo, qhi, masks), first, last = items[a]
                pt = PT[pslot[a]]
                for h2 in range(2):
                    gk = c // 4 + 2 * h2
                    ob = 4 + h2
                    S.add("pe", lambda e, h2=h2, gk=gk, ob=ob: e.matmul(
                        PSb[ob][:, qlo:qhi], VA.v[:, kt, gk, :], pt.v[:, h2, qlo:qhi],
                        start=first, stop=last, skip_group_check=True),
                        reads=VA.ki(kt, gk) + pt.k(), writes=PK(ob))
                if last:
                    for h2 in range(2):
                        S.add("dve", lambda e, h2=h2: e.tensor_copy(osb[h2].v, PSb[4 + h2]),
                              reads=PK(4 + h2), writes=osb[h2].k())

                    def normalise(c=c):
                        for h2 in range(2):
                            o_ = osb[h2]
                            rc_ = rec[h2]
                            col = (l // 2) * NH + c + 8 * h2
                            if use_sink or act_norm:
                                if use_sink:
                                    S.add("act", lambda e, o_=o_, col=col: e.activation(
                                        out=o_.v[HD:P, :], in_=o_.v[HD:P, :], func=AF.Ln, bias=es.v[HD:P, col:col + 1], scale=1.0),
                                        reads=o_.k() + es.k(), writes=o_.k())
                                else:
                                    S.add("act", lambda e, o_=o_: e.activation(
                                        out=o_.v[HD:P, :], in_=o_.v[HD:P, :], func=AF.Ln),
                                        reads=o_.k(), writes=o_.k())
                                S.add("act", lambda e, o_=o_, rc_=rc_: e.activation(
                                    out=rc_.v[0:HD, :], in_=o_.v[HD:P, :], func=AF.Exp, scale=-1.0),
                                    reads=o_.k(), writes=rc_.k())
                            else:
                                S.add("dve", lambda e, o_=o_, rc_=rc_: e.reciprocal(rc_.v[0:HD, :], o_.v[HD:P, :]),
                                      reads=o_.k(), writes=rc_.k())
                            S.add("dve", lambda e, o_=o_, rc_=rc_, h2=h2: e.tensor_tensor(
                                out=AT.v[h2 * HD:(h2 + 1) * HD, c, :], in0=o_.v[0:HD, :], in1=rc_.v[0:HD, :], op=ALU.mult),
                                reads=o_.k() + rc_.k(), writes=AT.ki(c))
                    pending.append((a + 3, normalise))

            n = len(items)
            for a in range(min(2, n)):
                s_mm(a)
            for a in range(n):
                do_exp(a)
                if a + 2 < n:
                    s_mm(a + 2)
                do_pv(a)
                while pending and pending[0][0] <= a:
                    pending.pop(0)[1]()
                if side and a in side:
                    for fn in side[a]:
                        fn()
            while pending:
                pending.pop(0)[1]()

        def oproj_m(i, l, m):
            bank = BX_ + m % 2
            for c in range(KC):
                S.add("pe", lambda e, c=c: e.matmul(
                    PSb[bank], Wo.v[:, c, m * P:(m + 1) * P], AT.v[:, c, :], start=(c == 0), stop=(c == KC - 1)),
                    reads=Wo.k(c * D + m * P, c * D + (m + 1) * P) + AT.ki(c), writes=PK(bank))
            residual(i, l, 1, m, bank)

        def oproj_tile(i, l):
            for m in range(KC):
                oproj_m(i, l, m)

        def full_jobs(nkt):
            return [(kt, kt * P, 0, TT, []) for kt in range(nkt)]

        def window_jobs(ti):
            jobs = [(kt, kt * P, 0, TT, []) for kt in range(4)]
            for R in range(ti * 4 - 1, ti * 4 + 5):
                if R < 0 or R >= NS // P:
                    continue
                rr = R - ti * 4
                blo, bhi = max(0, rr - 1), min(3, rr + 1)
                masks = []
                for b in range(blo, bhi + 1):
                    if rr == b - 1:
                        masks.append((b, "L"))
                    elif rr == b + 1:
                        masks.append((b, "R"))
                jobs.append((4 + R, CTX + R * P, blo * P, (bhi + 1) * P, masks))
            ctx_j, band_j = jobs[:4], jobs[4:]
            out = [ctx_j[0]]
            rest = ctx_j[1:]
            while band_j or rest:
                if band_j:
                    out.append(band_j.pop(0))
                if rest:
                    out.append(rest.pop(0))
            return out

        def prompt_jobs():
            return [(0, 0, 0, 256, []), (1, 128, 0, 256, []), (2, 256, 256, 512, []), (3, 384, 256, 512, [])]

        def attn_layer(l, lvl=9):
            odd = (l % 2 == 1)
            att_weights(l)
            load_ctx(l)
            nst = NS // TT
            load_x(0)
            rr = load_rope(0)
            norm_tile(0, l, 1, BY_)
            for i in range(nst):
                kv_tile(l, CTX + i * TT, 4 + i * 4, rr, None)
                if i + 1 < nst:
                    load_x(i + 1)
                    rr = load_rope((i + 1) * TT)
                    norm_tile(i + 1, l, 1, BY_)
            load_x(0)
            rr = load_rope(0)
            norm_tile(0, l, 1, BY_)
            q_tile_free(l, rr)
            for i in range(nst):
                jobs = window_jobs(i) if odd else full_jobs(NKT)
                nper = len(jobs)
                side = {}

                def at(c, frac, fn):
                    side.setdefault(c * nper + min(nper - 1, int(frac * nper)), []).append(fn)

                if i > 0:
                    for m in range(KC):
                        at(0, 0.05 + 0.05 * m, lambda m=m, i=i: oproj_m(i - 1, l, m))
                    at(0, 0.5, lambda i=i: store_x(i - 1))
                last_steps = None
                if i + 1 < nst:
                    box = {}

                    def ld(i=i, box=box):
                        load_x(i + 1)
                        box["r"] = load_rope((i + 1) * TT)
                    at(0, 0.55, ld)
                    rnext = (rcnt["r"]) % 2
                    nsteps = norm_steps(i + 1, l, 1, BY_, lean=True)
                    for n_, st in enumerate(nsteps):
                        at(0, 0.55 + 0.04 * n_, st)
                    for c in range(KC):
                        steps = q_chunk_steps(l, c, rnext)
                        if c < KC - 1:
                            for n_, st in enumerate(steps):
                                at(c + 1, 0.05 + 0.1 * n_, st)
                        else:
                            last_steps = steps
                attend(l, jobs, odd, side)
                if last_steps is not None:
                    run(last_steps)
            oproj_tile(nst - 1, l)
            store_x(nst - 1)
            for i in range(nst, NT):
                load_x(i)
            for i in range(nst, NT):
                norm_tile(i, l, 1, BY_)
                kv_tile(l, 0, 0, None, (i - nst) * 2)
                q_tile_free(l, None)
                attend(l, prompt_jobs(), odd, act_norm=True)
                oproj_tile(i, l)
                store_x(i)

        for l in range(depth):
            if "f1" in phases:
                ffn_pass(l, 0)
            for ph_ in phases:
                if ph_.startswith("att"):
                    attn_layer(l, int(ph_[3:]) if len(ph_) > 3 else 9)
            if "f2" in phases:
                ffn_pass(l, 1)

        A.top = base
        load_x(0)
        for i in range(NT):
            sl = xt[i % 2]
            st = stage[i % 2]
            if i + 1 < NT:
                load_x(i + 1)
            for s4 in range(4):
                for hf in range(2):
                    b = (s4 * 2 + hf) % 4
                    for kk in range(4):
                        k = hf * 4 + kk
                        S.add("pe", lambda e, sl=sl, k=k, kk=kk, s4=s4, b=b: e.transpose(
                            PSb[b][:, kk * P:(kk + 1) * P], sl.v[:, k, s4 * P:(s4 + 1) * P], ident_f),
                            reads=sl.ki(k) + identf.k(), writes=PK(b))
                    if hf == 0:
                        S.add("act", lambda e, st=st, s4=s4, b=b: e.copy(st.v[:, s4, 0:TT], PSb[b]),
                              reads=PK(b), writes=st.ki(s4))
                    else:
                        S.add("dve", lambda e, st=st, s4=s4, b=b: e.tensor_copy(st.v[:, s4, TT:D], PSb[b]),
                              reads=PK(b), writes=st.ki(s4))
            S.add("sp", lambda e, i=i, st=st: e.dma_start(
                out=y_d[i * TT:(i + 1) * TT, :].rearrange("(s p) d -> p s d", p=P), in_=st.v),
                reads=st.k(), writes=[("Y", i)], dma=True)

        S.number()
        block = ctx.enter_context(nc.Block())

        @block.tensor
        def _(e):
            S.emit("pe", e, sems, dsems)

        @block.scalar
        def _(e):
            S.emit("act", e, sems, dsems)

        @block.vector
        def _(e):
            S.emit("dve", e, sems, dsems)

        @block.gpsimd
        def _(e):
            S.emit("pool", e, sems, dsems)

        @block.sync
        def _(e):
            S.emit("sp", e, sems, dsems, final=True)

    return nc


def _consts():
    cst = np.zeros((P, 6, P), np.float32)
    idx = np.arange(P)
    cst[idx, 0, idx] = 1.0
    d = idx % HD
    e = d % 32
    partner = np.where(e < 16, idx + 16, idx - 16)
    cst[partner, 1, idx] = 1.0
    cst[:, 2, :] = (idx[:, None] // HD == idx[None, :] // HD) / float(HD)
    cst[:, 3, :] = 1.0 / D
    cst[:, 4, :] = np.where(idx[:, None] >= idx[None, :], 0.0, -30000.0)
    cst[:, 5, :] = np.where(idx[:, None] <= idx[None, :], 0.0, -30000.0)
    tok = np.arange(NS)
    row = (tok // 64).astype(np.float32)
    col = (tok % 64).astype(np.float32)
    freqs = (1.0 / np.power(np.float32(10000.0), np.arange(16, dtype=np.float32) / np.float32(16))).astype(np.float32)
    f = e % 16
    pos = np.where((d // 32 == 0)[:, None], row[None, :], col[None, :]).astype(np.float32)
    ang = (pos * freqs[f][:, None]).astype(np.float32)
    rc = np.cos(ang).astype(np.float32)
    sn = np.sin(ang).astype(np.float32)
    rs = np.where((e < 16)[:, None], -sn, sn).astype(np.float32)
    return cst, rc, rs


_NC_CACHE = {}


def _prep(x_prompt, x_sample, cache_k, cache_v, c, c_ctx, w_mod, b_mod, norm_g, w_qkv, w_o,
          q_norm_g, k_norm_g, sink, w_ffn_in, w_ffn_out):
    f32 = np.float32
    x_prompt = np.asarray(x_prompt, f32)
    x_sample = np.asarray(x_sample, f32)
    cache_k = np.asarray(cache_k, f32)
    cache_v = np.asarray(cache_v, f32)
    c = np.asarray(c, f32)
    c_ctx = np.asarray(c_ctx, f32)
    w_qkv = np.asarray(w_qkv, f32)
    w_o = np.asarray(w_o, f32)
    hq = np.concatenate([[cc, 8 + cc] for cc in range(8)])
    perm_q = (hq[:, None] * HD + np.arange(HD)[None, :]).reshape(-1)
    kvo = np.array([0, 2, 1, 3])
    perm_k = (kvo[:, None] * HD + np.arange(HD)[None, :]).reshape(-1)
    cols = np.concatenate([perm_q, D + perm_k, D + 256 + np.arange(256)])
    w_qkv_p = np.ascontiguousarray(w_qkv[:, :, cols])
    w_o_p = np.ascontiguousarray(w_o[:, perm_q, :])
    ck_p = np.ascontiguousarray(cache_k[:, :, :, kvo, :]).reshape(N_CORES, DEPTH, CTX, 256)
    cv_p = np.ascontiguousarray(cache_v).reshape(N_CORES, DEPTH, CTX, 256)
    cst, rc, rs = _consts()
    shared = {
        "w_mod": np.ascontiguousarray(np.asarray(w_mod, f32)),
        "b_mod": np.ascontiguousarray(np.asarray(b_mod, f32)),
        "norm_g": np.ascontiguousarray(np.asarray(norm_g, f32).reshape(DEPTH * 3, D)),
        "w_qkv": w_qkv_p,
        "w_o": w_o_p,
        "qk_g": np.ascontiguousarray(np.concatenate([np.asarray(q_norm_g, f32), np.asarray(k_norm_g, f32)], 0)),
        "sink": np.ascontiguousarray(np.asarray(sink, f32).reshape(1, 32)),
        "w_in": np.ascontiguousarray(np.asarray(w_ffn_in, f32)),
        "w_out": np.ascontiguousarray(np.asarray(w_ffn_out, f32)),
        "cst": cst, "rope_c": rc, "rope_s": rs,
    }
    in_maps = []
    for i in range(N_CORES):
        m = dict(shared)
        m["x"] = np.ascontiguousarray(np.concatenate(
            [x_sample[i], x_prompt[4 * i:4 * i + 4].reshape(NPR, D)], 0))
        m["ck"] = ck_p[i]
        m["cv"] = cv_p[i]
        m["cond"] = np.ascontiguousarray(np.stack([c[i], c_ctx], 0))
        in_maps.append(m)
    return in_maps


def kernel(x_prompt, x_sample, cache_k, cache_v, c, c_ctx, w_mod, b_mod, norm_g, w_qkv, w_o,
           q_norm_g, k_norm_g, sink, w_ffn_in, w_ffn_out):
    f32 = np.float32
    in_maps = _prep(x_prompt, x_sample, cache_k, cache_v, c, c_ctx, w_mod, b_mod, norm_g, w_qkv, w_o,
                    q_norm_g, k_norm_g, sink, w_ffn_in, w_ffn_out)
    if "nc" not in _NC_CACHE:
        _NC_CACHE["nc"] = build()
    nc = _NC_CACHE["nc"]
    res = run_bass_kernel_spmd(nc, in_maps, core_ids=list(range(N_CORES)))
    y_prompt = np.empty((32, 256, D), f32)
    y_sample = np.empty((8, NS, D), f32)
    new_k = np.empty((32, DEPTH, 256, NKV, HD), f32)
    new_v = np.empty((32, DEPTH, 256, NKV, HD), f32)
    for i in range(N_CORES):
        r = res.results[i]
        y = np.asarray(r["y"])
        y_sample[i] = y[:NS]
        y_prompt[4 * i:4 * i + 4] = y[NS:].reshape(4, 256, D)
        new_k[4 * i:4 * i + 4] = np.asarray(r["nk"]).reshape(4, DEPTH, 256, NKV, HD)
        new_v[4 * i:4 * i + 4] = np.asarray(r["nv"]).reshape(4, DEPTH, 256, NKV, HD)
    return (y_prompt, y_sample, new_k, new_v)
```
